# Optimizing a Trainium2 kernel written in Bass

```python
import jax, jax.numpy as jnp
from jax import lax
import numpy as np

D_MODEL = 1024
BATCH = 16
SEQ = 2048
DEPTH = 1

D_MIX = D_MODEL
A_HEADS = 8
A_HEAD_DIM = 64
D_A = A_HEADS * A_HEAD_DIM
DILATED_BRANCHES = ((128, 1), (512, 4), (2048, 16))
BLK = 128
B_HEADS = 8
B_NOPE_DIM = 64
B_ROPE_DIM = 32
B_V_DIM = 64
D_B = B_HEADS * B_V_DIM
Q_LORA = 384
KV_LORA = 256
ROPE_THETA = 10000.0
P_IN = 3 * D_A + Q_LORA + KV_LORA + B_ROPE_DIM
N_BUCKETS = 32
MAX_DISTANCE = 2048
D_FF = ((-(-8 * D_MODEL // 3) + 255) // 256) * 256
N_MOD = 6
EPS = 1e-6
NEG = -1e30

kernel_name = 'hybrid_dilated_mla_adaln_layer'


def _rmsnorm(x, g):
    xf = x.astype(jnp.float32)
    y = xf * lax.rsqrt(jnp.mean(xf * xf, axis=-1, keepdims=True) + EPS)
    return (y * g.astype(jnp.float32)).astype(x.dtype)


def _t5_bucket(dist):
    max_exact = N_BUCKETS // 2
    d = np.maximum(dist, 1).astype(np.float64)
    large = max_exact + (np.log(d / max_exact) / np.log(MAX_DISTANCE / max_exact)
                         * (N_BUCKETS - max_exact)).astype(np.int64)
    large = np.minimum(large, N_BUCKETS - 1)
    return np.where(dist < max_exact, dist, large).astype(np.int32)


def _dilated_branch(q, k, v, rel_bias, window, dilation):
    B, S, H, E = q.shape
    span = window // dilation
    n = S // dilation
    nb = -(-n // BLK)
    n_pad = nb * BLK

    def to_residue(t):
        t = t.reshape(B, n, dilation, H, E).transpose(0, 2, 3, 1, 4)
        return jnp.pad(t, ((0, 0), (0, 0), (0, 0), (0, n_pad - n), (0, 0)))

    def band(t):
        t = jnp.pad(to_residue(t), ((0, 0), (0, 0), (0, 0), (BLK, 0), (0, 0)))
        t = t.reshape(B, dilation, H, nb + 1, BLK, E)
        return jnp.concatenate([t[:, :, :, :-1], t[:, :, :, 1:]], axis=4)

    qb = to_residue(q).reshape(B, dilation, H, nb, BLK, E)
    kb, vb = band(k), band(v)
    a = np.arange(BLK)[:, None]
    bk = np.arange(2 * BLK)[None, :]
    steps = BLK + a - bk
    valid = (steps >= 0) & (steps <= span)
    first = valid & (bk >= BLK)
    mask = np.concatenate([first[None], np.broadcast_to(valid, (nb - 1,) + valid.shape)], axis=0)
    bucket = _t5_bucket(np.clip(steps, 0, span) * dilation)
    bias = jnp.transpose(rel_bias[bucket], (2, 0, 1)).astype(jnp.float32)

    logits = jnp.einsum('bdhnqe,bdhnke->bdhnqk', qb, kb,
                        preferred_element_type=jnp.float32) * (E ** -0.5)
    logits = logits + bias[None, None, :, None]
    logits = jnp.where(jnp.asarray(mask)[None, None, None], logits, NEG)
    m = jnp.max(logits, axis=-1, keepdims=True)
    p = jnp.exp(logits - m)
    s = jnp.sum(p, axis=-1, keepdims=True)
    o = jnp.einsum('bdhnqk,bdhnke->bdhnqe', p, vb.astype(jnp.float32)) / s
    lse = (m + jnp.log(s))[..., 0]

    def from_residue(t):
        t = t.reshape((B, dilation, H, n_pad) + t.shape[5:])[:, :, :, :n]
        t = jnp.moveaxis(t, 3, 1)
        return t.reshape((B, S, H) + t.shape[4:])

    return from_residue(o), from_residue(lse)


def _rope(t):
    S, R = t.shape[1], t.shape[-1]
    half = R // 2
    inv = ROPE_THETA ** (-jnp.arange(half, dtype=jnp.float32) / half)
    ang = jnp.arange(S, dtype=jnp.float32)[:, None] * inv[None, :]
    cos, sin = jnp.cos(ang)[None, :, None], jnp.sin(ang)[None, :, None]
    t1, t2 = t[..., :half].astype(jnp.float32), t[..., half:].astype(jnp.float32)
    return jnp.concatenate([t1 * cos - t2 * sin, t1 * sin + t2 * cos], axis=-1).astype(t.dtype)


def _mla_attention(q_nope, q_rope, k_nope, k_rope, v):
    B, S, H, _ = q_nope.shape
    nq = S // BLK
    scale = (B_NOPE_DIM + B_ROPE_DIM) ** -0.5
    qn = jnp.moveaxis(q_nope.reshape(B, nq, BLK, H, -1), 1, 0)
    qr = jnp.moveaxis(q_rope.reshape(B, nq, BLK, H, -1), 1, 0)
    kpos = jnp.arange(S)

    def block(args):
        qn_b, qr_b, i = args
        logits = (jnp.einsum('bqhe,bkhe->bhqk', qn_b, k_nope, preferred_element_type=jnp.float32)
                  + jnp.einsum('bqhe,bke->bhqk', qr_b, k_rope, preferred_element_type=jnp.float32)) * scale
        qpos = i * BLK + jnp.arange(BLK)
        logits = jnp.where(qpos[:, None] >= kpos[None, :], logits, NEG)
        p = jax.nn.softmax(logits, axis=-1)
        return jnp.einsum('bhqk,bkhe->bqhe', p.astype(v.dtype), v)

    o = lax.map(block, (qn, qr, jnp.arange(nq)))
    return jnp.moveaxis(o, 0, 1).reshape(B, S, H, -1)


def _mixer(h, w_in, g_cq, w_uq, g_ckv, w_ukv, rel_bias, g_out_a, g_out_b, w_out):
    B, S, _ = h.shape
    proj = h @ w_in
    i1, i2, i3 = D_A, 2 * D_A, 3 * D_A
    i4, i5 = i3 + Q_LORA, i3 + Q_LORA + KV_LORA
    qa, ka, va, cq, ckv, kr = jnp.split(proj, [i1, i2, i3, i4, i5], axis=-1)
    qa = qa.reshape(B, S, A_HEADS, A_HEAD_DIM)
    ka = ka.reshape(B, S, A_HEADS, A_HEAD_DIM)
    va = va.reshape(B, S, A_HEADS, A_HEAD_DIM)
    branches = [_dilated_branch(qa, ka, va, rel_bias, w, d) for (w, d) in DILATED_BRANCHES]
    o_stack = jnp.stack([br[0] for br in branches])
    lse = jnp.stack([br[1] for br in branches])
    wts = jax.nn.softmax(lse, axis=0)
    out_a = jnp.einsum('rbsh,rbshe->bshe', wts, o_stack).reshape(B, S, D_A).astype(h.dtype)
    q = (_rmsnorm(cq, g_cq) @ w_uq).reshape(B, S, B_HEADS, B_NOPE_DIM + B_ROPE_DIM)
    q_nope, q_rope = q[..., :B_NOPE_DIM], _rope(q[..., B_NOPE_DIM:])
    kv = (_rmsnorm(ckv, g_ckv) @ w_ukv).reshape(B, S, B_HEADS, B_NOPE_DIM + B_V_DIM)
    k_nope, v = kv[..., :B_NOPE_DIM], kv[..., B_NOPE_DIM:]
    k_rope = _rope(kr[:, :, None, :])[:, :, 0]
    out_b = _mla_attention(q_nope, q_rope, k_nope, k_rope, v).reshape(B, S, D_B)
    y = jnp.concatenate([_rmsnorm(out_a, g_out_a), _rmsnorm(out_b, g_out_b)], axis=-1)
    return y @ w_out


def _swiglu(h, w_ffn_in, w_ffn_out):
    g, u = jnp.split(h @ w_ffn_in, 2, axis=-1)
    return (jax.nn.silu(g) * u) @ w_ffn_out


def setup_inputs(seed: int = 0) -> dict:
    key = jax.random.key(seed)
    ks = jax.random.split(key, 20)
    f32 = jnp.float32

    def nrm(k, shape, scale):
        return jax.random.normal(k, shape, f32) * scale

    def gain(k, shape):
        return 1.0 + 0.05 * jax.random.normal(k, shape, f32)

    L = DEPTH
    return {
        'x': nrm(ks[0], (BATCH, SEQ, D_MODEL), 1.0),
        'c': nrm(ks[1], (BATCH, D_MODEL), 1.0),
        'w_ada': nrm(ks[2], (L, D_MODEL, N_MOD * D_MODEL), 0.5 * D_MODEL ** -0.5),
        'b_ada': nrm(ks[3], (L, N_MOD * D_MODEL), 0.01),
        'g_norm1': gain(ks[4], (L, D_MODEL)),
        'w_in': nrm(ks[5], (L, D_MODEL, P_IN), D_MODEL ** -0.5),
        'g_cq': gain(ks[6], (L, Q_LORA)),
        'w_uq': nrm(ks[7], (L, Q_LORA, B_HEADS * (B_NOPE_DIM + B_ROPE_DIM)), Q_LORA ** -0.5),
        'g_ckv': gain(ks[8], (L, KV_LORA)),
        'w_ukv': nrm(ks[9], (L, KV_LORA, B_HEADS * (B_NOPE_DIM + B_V_DIM)), KV_LORA ** -0.5),
        'rel_bias': nrm(ks[10], (N_BUCKETS, A_HEADS), 0.5),
        'g_out_a': gain(ks[11], (L, D_A)),
        'g_out_b': gain(ks[12], (L, D_B)),
        'w_out': nrm(ks[13], (L, D_MIX, D_MODEL), D_MIX ** -0.5),
        'g_norm2': gain(ks[14], (L, D_MODEL)),
        'w_ffn_in': nrm(ks[15], (L, D_MODEL, 2 * D_FF), D_MODEL ** -0.5),
        'w_ffn_out': nrm(ks[16], (L, D_FF, D_MODEL), D_FF ** -0.5),
        'g_final': gain(ks[17], (D_MODEL,)),
    }


def reference(x, c, w_ada, b_ada, g_norm1, w_in, g_cq, w_uq, g_ckv, w_ukv, rel_bias,
              g_out_a, g_out_b, w_out, g_norm2, w_ffn_in, w_ffn_out, g_final):
    cond = jax.nn.silu(c)
    for l in range(DEPTH):
        mod = (cond @ w_ada[l] + b_ada[l])[:, None, :]
        sh1, sc1, g1, sh2, sc2, g2 = jnp.split(mod, N_MOD, axis=-1)
        h = _rmsnorm(x, g_norm1[l]) * (1.0 + sc1) + sh1
        x = x + g1 * _mixer(h, w_in[l], g_cq[l], w_uq[l], g_ckv[l], w_ukv[l], rel_bias,
                            g_out_a[l], g_out_b[l], w_out[l])
        h = _rmsnorm(x, g_norm2[l]) * (1.0 + sc2) + sh2
        x = x + g2 * _swiglu(h, w_ffn_in[l], w_ffn_out[l])
    return _rmsnorm(x, g_final)
```

```python
import numpy as np
import concourse.bass as bass
import concourse.mybir as mybir
from concourse.bass_utils import run_bass_kernel_spmd
from contextlib import ExitStack

F32 = mybir.dt.float32
BF16 = mybir.dt.bfloat16
AF = mybir.ActivationFunctionType
ALU = mybir.AluOpType

ENGS = ("pe", "act", "dve", "pool", "sp")
S_LEN = 2048
D = 1024
DFF = 2816
EPS = 1e-6
NCORES = 8


class Region:
    __slots__ = ("name", "w", "r", "excl")

    def __init__(self, name, excl=False):
        self.name = name
        self.w = None
        self.r = []
        self.excl = excl


class Sched:
    def __init__(self, nc, n_dma_sems=48):
        self.nc = nc
        self.ops = {e: [] for e in ENGS}
        self.count = {e: 0 for e in ENGS}
        self.n_dma = n_dma_sems
        self.dma_use = [0] * n_dma_sems
        self.n_sw = 16
        self.dma_next = {"sw": 0, "hw": 0}
        self.same_engine_wait = {"pe": False, "act": True, "dve": True, "pool": True, "sp": False}
        self.pending = {e: set() for e in ENGS}

    def R(self, name):
        return Region(name)

    def barrier(self):
        toks = set()
        for e in ENGS:
            if self.count[e] > 0:
                toks.add((e, self.count[e]))
        for i in range(self.n_dma):
            if self.dma_use[i] > 0:
                toks.add((("d", i), 16 * self.dma_use[i]))
        for e in ENGS:
            self.pending[e] |= toks

    def op(self, eng, fn, reads=(), writes=(), dma=False):
        deps = set()
        if self.pending[eng]:
            deps |= self.pending[eng]
            self.pending[eng] = set()
        writes = list(writes) + [r for r in reads if r.excl]
        reads = [r for r in reads if not r.excl]
        for r in reads:
            if r.w is not None:
                deps.add(r.w)
        for w in writes:
            if w.w is not None:
                deps.add(w.w)
            deps.update(w.r)
        if dma:
            if eng == "pool":
                i = self.dma_next["sw"]
                self.dma_next["sw"] = (i + 1) % self.n_sw
            else:
                i = self.n_sw + self.dma_next["hw"]
                self.dma_next["hw"] = (self.dma_next["hw"] + 1) % (self.n_dma - self.n_sw)
            if self.dma_use[i] > 0:
                deps.add((("d", i), 16 * self.dma_use[i]))
            self.dma_use[i] += 1
            tok = (("d", i), 16 * self.dma_use[i])
        else:
            self.count[eng] += 1
            tok = (eng, self.count[eng])
        for r in reads:
            r.r.append(tok)
        for w in writes:
            w.w = tok
            w.r = []
        self.ops[eng].append((fn, deps, tok, dma))
        return tok

    def emit(self, final_waits=()):
        nc = self.nc
        with ExitStack() as st:
            sems = {}
            for e in ENGS:
                sems[e] = st.enter_context(nc.semaphore("s_" + e))
            for i in range(self.n_dma):
                sems[("d", i)] = st.enter_context(nc.semaphore("sd%d" % i))
            block = st.enter_context(nc.Block())

            def run(ename, eng):
                seen = {}
                for fn, deps, tok, dma in self.ops[ename]:
                    need = {}
                    for (s, v) in deps:
                        if s == ename and not self.same_engine_wait[ename]:
                            continue
                        if seen.get(s, 0) >= v:
                            continue
                        if need.get(s, 0) < v:
                            need[s] = v
                    for s, v in need.items():
                        eng.wait_ge(sems[s], v)
                        seen[s] = v
                    ins = fn(eng)
                    if dma:
                        ins.then_inc(sems[tok[0]], 16)
                    else:
                        ins.then_inc(sems[ename], 1)
                if ename == "sp":
                    for (s, v) in final_waits:
                        eng.wait_ge(sems[s], v)

            @block.tensor
            def _(eng):
                run("pe", eng)

            @block.scalar
            def _(eng):
                run("act", eng)

            @block.vector
            def _(eng):
                run("dve", eng)

            @block.gpsimd
            def _(eng):
                run("pool", eng)

            @block.sync
            def _(eng):
                run("sp", eng)


class Arena:
    def __init__(self, nc, total_bytes):
        self.t = nc.alloc_sbuf_tensor("arena", [128, total_bytes // 4], F32)
        self.top = 0
        self.total = total_bytes
        self.peak = 0

    def alloc(self, shape, dtype, at=None):
        n = int(np.prod(shape))
        bpe = 4 if dtype == F32 else 2
        nbytes = (n * bpe + 63) // 64 * 64
        if at is None:
            off = self.top
            self.top += nbytes
            self.peak = max(self.peak, self.top)
            assert self.top <= self.total, ("SBUF arena overflow", self.top, self.total)
        else:
            off = at
        self.last_off = off
        w0 = off // 4
        if dtype == F32:
            ap = self.t[:, w0:w0 + n]
        else:
            ap = self.t[:, w0:w0 + n // 2].bitcast(BF16)
        if len(shape) == 2:
            ap = ap.rearrange("p (a b) -> p a b", a=shape[0])
        elif len(shape) == 3:
            ap = ap.rearrange("p (a b c) -> p a b c", a=shape[0], b=shape[1])
        return ap


def build(debug=False):
    nc = bass.Bass("TRN2", target_bir_lowering=False)

    def din(name, shape, dt=F32):
        return nc.dram_tensor(name, list(shape), dt, kind="ExternalInput").ap()

    d_x = din("xT", [2, 8, 128, S_LEN])
    d_c = din("cT", [128, 8, 2])
    d_wada = din("w_ada", [128, 6, 8, 1024])
    d_bada = din("b_adaT", [128, 48])
    d_gvec = din("gvec", [128, 40])
    d_win = din("w_in", [128, 8, 2240])
    d_wuq = din("w_uq", [128, 3, 1024])
    d_wukv = din("w_ukv", [128, 2, 1024])
    d_relb = din("rel_bias", [32, 8])
    d_wout = din("w_out", [128, 8, 8, 128])
    d_wffi = din("w_ffn_in", [128, 22, 8, 256])
    d_wffo = din("w_ffn_out", [128, 8, 22, 128])
    d_const = din("consts", [128, 512])
    d_onehot = din("onehot", [32, S_LEN])
    d_rope = din("rope", [128, 2, S_LEN])
    d_out = nc.dram_tensor("outT", [2, 8, 128, S_LEN], F32, kind="ExternalOutput").ap()
    WP = 2176
    d_wpad_h = nc.dram_tensor("wpad", [8, WP], BF16)
    d_wpad = d_wpad_h.ap()
    d_wtab = nc.dram_tensor("wtab", [128, 8, S_LEN], BF16).ap()
    dbg = {}
    if debug:
        for nm, shp, dt in [("dbg_h", [128, 8, 512], BF16), ("dbg_qa", [128, 4, S_LEN], BF16),
                            ("dbg_va", [128, 16, 8, 65], BF16), ("dbg_cqn", [128, 3, S_LEN], BF16),
                            ("dbg_kr", [128, S_LEN], BF16), ("dbg_y", [128, 8, S_LEN], BF16),
                            ("dbg_x1", [128, 8, 1024], F32), ("dbg_mod", [128, 48, 2], F32),
                            ("dbg_w", [128, 8, S_LEN], BF16), ("dbg_qt", [128, 8, S_LEN], BF16),
                            ("dbg_kt", [128, 8, S_LEN], BF16)]:
            dbg[nm] = nc.dram_tensor(nm, shp, dt, kind="ExternalOutput").ap()

    S = Sched(nc)
    A = Arena(nc, 207 * 1024)
    banks = [nc.alloc_psum_tensor("psb%d" % i, [128, 512], F32) for i in range(7)]
    bankT = nc.alloc_psum_tensor("psT", [128, 1024], BF16)
    Rb = [Region("bank%d" % i, excl=True) for i in range(7)]
    RbT = Region("bankT", excl=True)
    stores = []

    def MM(out, lhsT, rhs, start, stop, reads, writes):
        return S.op("pe", lambda e: e.matmul(out, lhsT, rhs, start=start, stop=stop), reads, writes)

    def ACT(out, in_, func, reads, writes, bias=None, scale=None, accum_out=None):
        kw = {}
        if bias is not None:
            kw["bias"] = bias
        if scale is not None:
            kw["scale"] = scale
        if accum_out is not None:
            kw["accum_out"] = accum_out
        return S.op("act", lambda e: e.activation(out=out, in_=in_, func=func, **kw), reads, writes)

    def TT(eng, out, in0, in1, op, reads, writes):
        return S.op(eng, lambda e: e.tensor_tensor(out=out, in0=in0, in1=in1, op=op), reads, writes)

    def TS(eng, out, in0, s1, s2, op0, op1, reads, writes):
        return S.op(eng, lambda e: e.tensor_scalar(out=out, in0=in0, scalar1=s1, scalar2=s2, op0=op0, op1=op1),
                    reads, writes)

    def STT(out, in0, scalar, in1, op0, op1, reads, writes, accum_out=None):
        kw = {}
        if accum_out is not None:
            kw["accum_out"] = accum_out
        return S.op("dve", lambda e: e.scalar_tensor_tensor(out=out, in0=in0, scalar=scalar, in1=in1,
                                                            op0=op0, op1=op1, **kw), reads, writes)

    def CP(eng, out, in_, reads, writes):
        if eng == "act":
            return S.op("act", lambda e: e.copy(out=out, in_=in_), reads, writes)
        return S.op(eng, lambda e: e.tensor_copy(out=out, in_=in_), reads, writes)

    def RECIP(out, in_, reads, writes):
        return S.op("dve", lambda e: e.reciprocal(out=out, in_=in_), reads, writes)

    def TR(out, in_, reads, writes):
        return S.op("pe", lambda e: e.transpose(out, in_, ident), reads, writes)

    def DMA(eng, out, in_, reads, writes):
        return S.op(eng, lambda e: e.dma_start(out=out, in_=in_), reads, writes, dma=True)

    gen_i = [0]

    def gen_bank():
        i = gen_i[0] % 6
        gen_i[0] += 1
        return banks[i], Rb[i]

    cst = A.alloc([512], BF16)
    ident, ones_bf, tri, Jx = cst[:, 0:128], cst[:, 128:256], cst[:, 256:384], cst[:, 384:512]
    Rcst = S.R("cst")
    modT = A.alloc([48, 2], F32)
    Rmod = S.R("modT")
    gvec = A.alloc([40], F32)
    Rgv = S.R("gvec")
    geff = A.alloc([2, 8, 2], F32)
    zero1 = A.alloc([2], F32)
    epsb = A.alloc([2], F32)
    condT = A.alloc([8, 2], F32)
    badaT = A.alloc([48], F32)
    Rsmall = S.R("small")
    id32 = A.alloc([128], F32)
    condB = A.alloc([8, 2], F32)
    m32 = A.alloc([1024], F32)
    id2 = A.alloc([2], F32)
    yT = A.alloc([8, S_LEN], BF16)
    yT_off = A.last_off
    RyT = S.R("yT")
    M0 = A.top

    DMA("pool", cst, d_const, [], [Rcst])
    DMA("sp", id32, d_const[:, 0:128], [], [Rcst])
    DMA("sp", gvec, d_gvec, [], [Rgv])
    DMA("sp", badaT, d_bada, [], [Rsmall])
    DMA("sp", condT, d_c, [], [Rsmall])
    S.op("dve", lambda e: e.memset(zero1, 0.0), [], [Rsmall])
    S.op("dve", lambda e: e.memset(epsb, EPS), [], [Rsmall])
    epsap = epsb[:, 0:1]
    zap = zero1[:, 0:1]

    WIN_PIECES = [(0, 512), (512, 1024), (1024, 1536), (1536, 2240)]
    WIN_OFF = M0 + 8192 + 12288 + 8192 + 4096 + 16384 + 32768 + 16640
    w_in0 = A.alloc([8, 2240], BF16, at=WIN_OFF)
    Rwin0 = [S.R("w_in_p%d" % i) for i in range(4)]
    for i_, (c0_, c1_) in enumerate(WIN_PIECES):
        DMA("pool", w_in0[:, :, c0_:c1_], d_win[:, :, c0_:c1_], [], [Rwin0[i_]])
    wa = [A.alloc([8, 1024], F32) for _ in range(2)]
    Rwa = [S.R("wa0"), S.R("wa1")]
    RcondS = S.R("condS")
    Rm32 = S.R("m32")
    assert A.top <= WIN_OFF, (A.top, WIN_OFF)
    A.top = WIN_OFF + 8 * 2240 * 2
    relb = A.alloc([8], F32)
    oneh = A.alloc([S_LEN], F32)
    wv = A.alloc([WP], BF16)
    wtp = [A.alloc([S_LEN], BF16) for _ in range(2)]
    wts = [A.alloc([S_LEN], BF16) for _ in range(2)]
    Rtz = S.R("tz")
    Rwpad = S.R("wpad")
    Rwtab = S.R("wtab")
    Rwtp = [S.R("wtp0"), S.R("wtp1")]
    Rwts = [S.R("wts0"), S.R("wts1")]
    DMA("sp", id2[0:2], d_const[0:2, 0:2], [], [Rm32])
    DMA("sp", relb[0:32], d_relb, [], [Rtz])
    DMA("sp", oneh[0:32], d_onehot, [], [Rtz])
    ACT(condB, condT, AF.Silu, [Rsmall], [RcondS])

    def mod_finish(j, halves):
        for half, (bk, Rk) in enumerate(halves):
            CP("act" if half else "dve", m32[0:2, half * 512:(half + 1) * 512], bk[0:2, :], [Rk], [Rm32])
        mod_tail(j)

    def mod_slot(j):
        w, Rw = wa[j % 2], Rwa[j % 2]
        DMA("sp", w, d_wada[:, j], [], [Rw])
        for half in range(2):
            bk, Rk = gen_bank()
            for kc in range(8):
                MM(bk[0:2, :], condB[:, kc, :], w[:, kc, half * 512:(half + 1) * 512], kc == 0, kc == 7,
                   [Rw, RcondS], [Rk])
            CP("act" if half else "dve", m32[0:2, half * 512:(half + 1) * 512], bk[0:2, :], [Rk], [Rm32])
        mod_tail(j)

    def mod_tail(j):
        bkT_, RkT_ = gen_bank()
        for f in range(8):
            S.op("pe", (lambda f=f, bkT_=bkT_: (lambda e: e.transpose(bkT_[:, 2 * f:2 * f + 2],
                                                                    m32[0:2, f * 128:(f + 1) * 128], id2[0:2, 0:2])))(),
                 [Rm32], [RkT_])
        TT("dve", modT[:, j * 8:(j + 1) * 8, :], bkT_[:, 0:16].rearrange("p (f b) -> p f b", b=2),
           badaT[:, j * 8:(j + 1) * 8].unsqueeze(2).to_broadcast([128, 8, 2]), ALU.add, [RkT_, Rsmall], [Rmod])

    def toe_setup():
        ACT(relb[0:32], relb[0:32], AF.Exp, [Rtz], [Rtz])
        S.op("dve", lambda e: e.memset(wv[0:8, 0:128], 0.0), [], [Rtz])
        for q in range(4):
            bk, Rk = gen_bank()
            MM(bk[0:8, :], relb[0:32, :], oneh[0:32, q * 512:(q + 1) * 512], True, True, [Rtz], [Rk])
            CP("dve", wv[0:8, 127 + q * 512:127 + (q + 1) * 512], bk[0:8, :], [Rk], [Rtz])
        DMA("act", d_wpad[:, 0:2175], wv[0:8, 0:2175], [Rtz], [Rwpad])

    def toe_load(hh):
        src = bass.AP(tensor=d_wpad_h, offset=hh * WP, ap=[[1, 128], [1, S_LEN]])
        DMA("act", wtp[hh % 2], src, [Rwpad], [Rwtp[hh % 2]])

    def toe_head(hh):
        for q in range(4):
            bk, Rk = gen_bank()
            MM(bk[:, :], Jx, wtp[hh % 2][:, q * 512:(q + 1) * 512], True, True, [Rcst, Rwtp[hh % 2]], [Rk])
            CP("dve" if q % 2 else "act", wts[hh % 2][:, q * 512:(q + 1) * 512], bk[:, :], [Rk], [Rwts[hh % 2]])
        if hh + 2 < 8:
            toe_load(hh + 2)
        DMA("act", d_wtab[:, hh, :], wts[hh % 2], [Rwts[hh % 2]], [Rwtab])

    def make_geff(n, gofs, scslot):
        TS("dve", geff[:, n], modT[:, scslot * 8:(scslot + 1) * 8, :], 1.0, None, ALU.add, ALU.bypass,
           [Rmod], [Rsmall])
        TT("dve", geff[:, n], geff[:, n], gvec[:, gofs:gofs + 8].unsqueeze(2).to_broadcast([128, 8, 2]), ALU.mult,
           [Rsmall, Rgv], [Rsmall])

    toe_setup()
    toe_load(0)
    toe_load(1)
    toe_head(0)
    mod_slot(0)
    toe_head(1)
    mod_slot(1)
    for i_ in range(2, 8):
        toe_head(i_)
    make_geff(0, 0, 1)
    S.barrier()
    A.top = M0

    def fm_rstd(sq_list, nfeat, sq_reads):
        bk, Rk = gen_bank()
        n = len(sq_list)
        for i, sq in enumerate(sq_list):
            MM(bk[:, :], ones_bf, sq, i == 0, i == n - 1, [Rcst] + sq_reads, [Rk])
        return bk[:, :], Rk

    for b in range(2):
        rope = A.alloc([2, S_LEN], BF16)
        Rrope = S.R("rope")
        cqnT = A.alloc([3, S_LEN], BF16)
        ckvnT = A.alloc([2, S_LEN], BF16)
        KR = A.alloc([S_LEN], BF16)
        Rlat = S.R("lat")
        Rkr = S.R("kr")
        M1 = A.top
        QA = A.alloc([4, S_LEN], BF16)
        KA = A.alloc([8, S_LEN], BF16)
        VA = A.alloc([16, 8, 65], BF16)
        RQA, RKA, RVA = S.R("QA"), S.R("KA"), S.R("VA")
        M2 = A.top
        w_in = A.alloc([8, 2240], BF16)
        assert A.last_off == WIN_OFF, (A.last_off, WIN_OFF)
        RwinP = Rwin0 if b == 0 else [S.R("w_in_b1p%d" % i) for i in range(4)]
        xc = A.alloc([8, 512], F32, at=yT_off)
        Rxc = S.R("xc")
        sq = A.alloc([8, 512], BF16, at=yT_off + 16384)
        Rsq = S.R("sq")
        hT = A.alloc([8, 512], BF16, at=yT_off + 24576)
        RhT = S.R("hT")
        tmp = [A.alloc([512], F32) for _ in range(2)]
        Rtmp = [S.R("tmp0"), S.R("tmp1")]
        rl = A.alloc([512], F32)
        rstd = A.alloc([512], F32)
        Rrl, Rrstd = S.R("rl"), S.R("rstd")
        lat32 = A.alloc([3, 512], F32)
        sql = A.alloc([3, 512], BF16)
        Rlat32, Rsql = S.R("lat32"), S.R("sql")
        t1, t2 = tmp[0], tmp[1]
        Rt1, Rt2 = Rtmp[0], Rtmp[1]
        piece = A.alloc([1024], F32)
        Rpiece = S.R("piece")

        def mod_late_gen():
            bank7 = bankT[:, :].bitcast(F32)
            for j in range(2, 6):
                for kc in range(8):
                    DMA("sp", piece, d_wada[:, j, kc, :], [], [Rpiece])
                    MM(banks[6][0:2, :], condB[:, kc, :], piece[:, 0:512], kc == 0, kc == 7, [Rpiece, RcondS], [Rb[6]])
                    MM(bank7[0:2, :], condB[:, kc, :], piece[:, 512:1024], kc == 0, kc == 7, [Rpiece, RcondS], [RbT])
                    yield
                mod_finish(j, [(banks[6], Rb[6]), (bank7, RbT)])
                yield

        DMA("pool", rope, d_rope, [], [Rrope])
        if b > 0:
            for i_, (c0_, c1_) in enumerate(WIN_PIECES):
                DMA("pool", w_in[:, :, c0_:c1_], d_win[:, :, c0_:c1_], [], [RwinP[i_]])
        S.op("pool", lambda e: e.memset(VA[:, :, :, 64:65], 1.0), [], [RVA])
        for hh in range(8):
            lo = 64 if hh % 2 == 0 else 0
            S.op("pool", (lambda hh=hh, lo=lo: (lambda e: e.memset(KA[lo:lo + 64, hh, :], 0.0)))(), [], [RKA])

        def norm_gen(xin, Rxin, nrm, out_fn, Rout, shift_slot, T):
            sq_, Rsq_, rl_, Rrl_, rstd_, Rrstd_, tmp_, Rtmp_ = T
            ACT(sq_, xin, AF.Square, [Rxin], [Rsq_])
            yield
            bk, Rk = fm_rstd([sq_[:, kc, :] for kc in range(8)], D, [Rsq_])
            yield
            ACT(rl_, bk, AF.Ln, [Rk, Rsmall], [Rrl_], bias=epsap, scale=1.0 / D)
            ACT(rstd_, rl_, AF.Exp, [Rrl_], [Rrstd_], scale=-0.5)
            yield
            for kc in range(8):
                tm, Rtm = tmp_[kc % 2], Rtmp_[kc % 2]
                TT("dve", tm, xin[:, kc, :], rstd_, ALU.mult, [Rxin, Rrstd_], [Rtm])
                if nrm < 2:
                    sh = modT[:, shift_slot * 8 + kc, b:b + 1]
                    ACT(out_fn(kc), tm, AF.Identity, [Rtm, Rsmall, Rmod], [Rout], bias=sh,
                        scale=geff[:, nrm, kc, b:b + 1])
                else:
                    o_, Ro_, after = out_fn(kc)
                    ACT(o_, tm, AF.Identity, [Rtm, Rgv], [Ro_], scale=gvec[:, 16 + kc:17 + kc])
                    after()
                yield

        def run_all(g):
            for _ in g:
                pass

        def norm_chunk(xin, Rxin, nrm, out_fn, Rout, shift_slot):
            run_all(norm_gen(xin, Rxin, nrm, out_fn, Rout, shift_slot, (sq, Rsq, rl, Rrl, rstd, Rrstd, tmp, Rtmp)))

        hTs = [hT, A.alloc([8, 512], BF16)]
        RhTs = [RhT, S.R("hT1")]
        rlX = A.alloc([512], F32)
        rstdX = A.alloc([512], F32)
        tmpX = [A.alloc([512], F32) for _ in range(2)]
        TX = (sq, Rsq, rlX, S.R("rlX"), rstdX, S.R("rstdX"), tmpX, [S.R("tmpX0"), S.R("tmpX1")])

        def p1_norm(t):
            ts_ = slice(t * 512, (t + 1) * 512)
            DMA("sp", xc, d_x[b, :, :, ts_].rearrange("k p t -> p k t"), [], [Rxc])
            h_, Rh_ = hTs[t % 2], RhTs[t % 2]
            return norm_gen(xc, Rxc, 0, lambda kc: h_[:, kc, :], Rh_, 0, TX)

        def p1_groups(t, hT, RhT):
            ts = slice(t * 512, (t + 1) * 512)
            G = []

            def g_qk(gi, i):
                Rwin = RwinP[gi]
                dst, Rd, cofs = (QA, RQA, 0) if gi == 0 else (KA, RKA, 512)
                bk, Rk = gen_bank()
                for kc in range(8):
                    MM(bk[:, :], w_in[:, kc, cofs + i * 128:cofs + (i + 1) * 128], hT[:, kc, :], kc == 0, kc == 7,
                       [Rwin, RhT], [Rk])
                if gi == 0:
                    CP("act", dst[:, i, ts], bk[:, :], [Rk], [Rd])
                else:
                    CP("dve", KA[0:64, 2 * i, ts], bk[0:64, :], [Rk], [Rd])
                    CP("dve", KA[64:128, 2 * i + 1, ts], bk[64:128, :], [Rk], [Rd])

            def g_lat(i):
                Rwin = RwinP[3]
                cofs = 1536 + i * 128
                bk, Rk = gen_bank()
                for kc in range(8):
                    MM(bk[:, :], w_in[:, kc, cofs:cofs + 128], hT[:, kc, :], kc == 0, kc == 7, [Rwin, RhT], [Rk])
                CP("dve", lat32[:, i % 3, :], bk[:, :], [Rk], [Rlat32])
                ACT(sql[:, i % 3, :], bk[:, :], AF.Square, [Rk], [Rsql])

            def g_latnorm(lo, n, dst, gofs):
                bk, Rk = fm_rstd([sql[:, i, :] for i in range(n)], n * 128, [Rsql])
                ACT(rl, bk, AF.Ln, [Rk, Rsmall], [Rrl], bias=epsap, scale=1.0 / (n * 128))
                ACT(rstd, rl, AF.Exp, [Rrl], [Rrstd], scale=-0.5)
                for i in range(n):
                    tm, Rtm = tmp[i % 2], Rtmp[i % 2]
                    TT("dve", tm, lat32[:, i, :], rstd, ALU.mult, [Rlat32, Rrstd], [Rtm])
                    TS("pool", dst[:, i, ts], tm, gvec[:, gofs + i:gofs + i + 1], zap, ALU.mult, ALU.add,
                       [Rtm, Rgv, Rsmall], [Rlat])

            def g_kr():
                Rwin = RwinP[3]
                bkA, RkA = gen_bank()
                for kc in range(8):
                    MM(bkA[64:96, :], w_in[:, kc, 2176:2208], hT[:, kc, :], kc == 0, kc == 7, [Rwin, RhT], [RkA])
                bkB, RkB = gen_bank()
                for kc in range(8):
                    MM(bkB[64:96, :], w_in[:, kc, 2208:2240], hT[:, kc, :], kc == 0, kc == 7, [Rwin, RhT], [RkB])
                TT("dve", t1[64:96], bkA[64:96, :], rope[64:96, 0, ts], ALU.mult, [RkA, Rrope], [Rt1])
                TT("dve", t2[64:96], bkB[64:96, :], rope[64:96, 1, ts], ALU.mult, [RkB, Rrope], [Rt2])
                TT("pool", KR[64:96, ts], t1[64:96], t2[64:96], ALU.add, [Rt1, Rt2], [Rkr])

            def g_va(i):
                Rwin = RwinP[2]
                bk, Rk = gen_bank()
                for kc in range(8):
                    MM(bk[:, :], hT[:, kc, i * 128:(i + 1) * 128], w_in[:, kc, 1024:1536], kc == 0, kc == 7,
                       [Rwin, RhT], [Rk])
                CP("act", VA[:, 4 * t + i, :, 0:64], bk[:, :].rearrange("p (h e) -> p h e", e=64), [Rk], [RVA])

            for gi in range(2):
                for i in range(4):
                    G.append((lambda gi=gi, i=i: g_qk(gi, i)))
            for i in range(3):
                G.append((lambda i=i: g_lat(i)))
            G.append(lambda: g_latnorm(0, 3, cqnT, 32))
            for i in range(3, 5):
                G.append((lambda i=i: g_lat(i)))
            G.append(lambda: g_latnorm(3, 2, ckvnT, 35))
            G.append(g_kr)
            for i in range(4):
                G.append((lambda i=i: g_va(i)))
            return G

        run_all(p1_norm(0))
        if debug and b == 0:
            stores.append(DMA("sp", dbg["dbg_h"], hTs[0], [RhTs[0]], []))
        late = mod_late_gen() if b == 0 else None
        for t in range(4):
            nxt = p1_norm(t + 1) if t + 1 < 4 else None
            for gi_, g in enumerate(p1_groups(t, hTs[t % 2], RhTs[t % 2])):
                g()
                if nxt is not None:
                    next(nxt, None)
                if late is not None and gi_ % 2 == 1:
                    next(late, None)
            if nxt is not None:
                run_all(nxt)
        if late is not None:
            run_all(late)
            make_geff(1, 8, 4)
            if debug:
                stores.append(DMA("sp", dbg["dbg_mod"], modT, [Rmod], []))
        if debug and b == 0:
            stores.append(DMA("sp", dbg["dbg_qa"], QA, [RQA], []))
            stores.append(DMA("sp", dbg["dbg_va"], VA, [RVA], []))
            stores.append(DMA("sp", dbg["dbg_cqn"], cqnT, [Rlat], []))
            stores.append(DMA("sp", dbg["dbg_kr"], KR, [Rkr], []))
        S.barrier()
        A.top = M2

        def attention(grp, qT, kT, vaug, Rq, Rk_, Rv, scale, Wtab, RW, O, RO, PT, RPT, rec, Rrec, On, ROn, ss, Rss):
            NPT = len(PT)
            SB = [0, 1, 6]
            L = 3
            tiles = [(c, h, j) for c in range(4) for h in range(8) for j in range(4 * c + 4)]
            N = len(tiles)
            info = {}
            deferred = []

            def emit_S(idx):
                c, h, j = tiles[idx]
                q0 = max(4 * c, j)
                off = (q0 - 4 * c) * 128
                n = (4 * c + 4 - q0) * 128
                sbi = SB[idx % 3]
                sb, Rsb = banks[sbi], Rb[sbi]
                MM(sb[:, off:off + n], kT(h, j), qT(h, q0 * 128, n), True, True, list(Rq) + list(Rk_), [Rsb])
                pt, Rpt = PT[idx % NPT], RPT[idx % NPT]
                ACT(pt[:, off:off + n], sb[:, off:off + n], AF.Exp, [Rsb], [Rpt], scale=scale)
                if grp == 0:
                    d0 = (q0 - j) * 128
                    TT("dve", pt[:, off:off + n], pt[:, off:off + n], Wtab[:, h, d0:d0 + n], ALU.mult,
                       [Rpt, RW], [Rpt])
                elif j >= 4 * c:
                    TT("dve", pt[:, off:off + 128], pt[:, off:off + 128], tri, ALU.mult, [Rpt, Rcst], [Rpt])
                info[idx] = (pt, Rpt, q0)

            def stage_b(c, ii):
                def fn():
                    tile_ = 4 * c + ii
                    for m in range(4):
                        TR(bankT[:, m * 128:(m + 1) * 128], On[ii][:, m * 128:(m + 1) * 128], [ROn[ii], Rcst], [RbT])
                    for m in range(4):
                        TS("dve", yT[:, grp * 4 + m, tile_ * 128:(tile_ + 1) * 128], bankT[:, m * 128:(m + 1) * 128],
                           gvec[:, 24 + grp * 4 + m:25 + grp * 4 + m], None, ALU.mult, ALU.bypass, [RbT, Rgv], [RyT])
                return fn

            now = [0]

            def emit_PV(idx):
                c, h, j = tiles[idx]
                pt, Rpt, q0 = info.pop(idx)
                ch = c * 8 + h
                ab = 2 + (ch % 2)
                off = (q0 - 4 * c) * 128
                MM(banks[ab][0:65, off:512], vaug(h, j), pt[:, off:512], j == 0, j == 4 * c + 3, [Rpt, Rv], [Rb[ab]])
                if j == 4 * c + 3:
                    ot, Rot = OT[ch % 2], ROT[ch % 2]
                    CP("act", ot[0:65, :], banks[ab][0:65, :], [Rb[ab]], [Rot])

                    def fin(c=c, h=h, ch=ch, ot=ot, Rot=Rot):
                        Ob, ROb = O[c % 2], RO[c % 2]
                        xb = 4 + (ch % 2)
                        for ii in range(4):
                            S.op("pe", (lambda ii=ii: (lambda e: e.transpose(
                                banks[xb][:, ii * 65:(ii + 1) * 65], ot[0:65, ii * 128:(ii + 1) * 128], id32[0:65, 0:65])))(),
                                [Rot, Rcst], [Rb[xb]])
                        CP("dve", Ob[:, :, h, :], banks[xb][:, 0:260].rearrange("p (i e) -> p i e", e=65), [Rb[xb]], [ROb])
                        if h == 7:
                            for ii in range(4):
                                s_ = ss[:, ii, :]
                                r8, Rr8 = rec[ii], Rrec[ii]
                                of_, Rof_ = Of[ii % 2], ROf[ii % 2]
                                RECIP(r8, Ob[:, ii, :, 64], [ROb], [Rr8])
                                TT("dve", of_.rearrange("p (h e) -> p h e", e=64), Ob[:, ii, :, 0:64],
                                   r8.unsqueeze(2).to_broadcast([128, 8, 64]), ALU.mult, [ROb, Rr8], [Rof_])
                                STT(On[ii], of_, 1.0, of_, ALU.mult, ALU.mult, [Rof_], [ROn[ii], Rss[ii]],
                                    accum_out=s_[:, 0:1])
                                ACT(s_[:, 1:2], s_[:, 0:1], AF.Ln, [Rss[ii], Rsmall], [Rss[ii]], bias=epsap, scale=1.0 / 512)
                                ACT(s_[:, 2:3], s_[:, 1:2], AF.Exp, [Rss[ii]], [Rss[ii]], scale=-0.5)
                                TS("dve", On[ii], of_, s_[:, 2:3], None, ALU.mult, ALU.bypass, [Rof_, Rss[ii]], [ROn[ii]])
                                deferred.append((now[0] + 3 + 2 * ii, stage_b(c, ii)))
                            deferred.sort(key=lambda x_: x_[0])
                    deferred.append((idx + L + 2, fin))
                    deferred.sort(key=lambda x_: x_[0])

            for idx in range(N + L):
                now[0] = idx
                if idx < N:
                    emit_S(idx)
                if idx >= L:
                    emit_PV(idx - L)
                while deferred and deferred[0][0] <= idx:
                    deferred.pop(0)[1]()
            while deferred:
                now[0] += 1
                deferred.pop(0)[1]()

        Of, ROf = [None, None], [None, None]
        OT, ROT = [None, None], [None, None]

        def attn_temps():
            O = [A.alloc([4, 8, 65], F32) for _ in range(2)]
            PT = [A.alloc([512], BF16) for _ in range(5)]
            rec = [A.alloc([8], F32) for _ in range(8)]
            On = [A.alloc([512], BF16) for _ in range(4)]
            ss = A.alloc([4, 4], F32)
            Of[:] = [A.alloc([512], F32) for _ in range(2)]
            ROf[:] = [S.R("Of0"), S.R("Of1")]
            OT[:] = [A.alloc([512], F32) for _ in range(2)]
            ROT[:] = [S.R("OT0"), S.R("OT1")]
            return (O, [S.R("O0"), S.R("O1")], PT, [S.R("pt%d" % i) for i in range(5)], rec,
                    [S.R("rec%d" % i) for i in range(8)], On, [S.R("On%d" % i) for i in range(4)], ss,
                    [S.R("ss%d" % i) for i in range(4)])

        Wtab = A.alloc([8, S_LEN], BF16)
        RW = S.R("Wtab")
        for hh in range(8):
            DMA("sp", Wtab[:, hh, :], d_wtab[:, hh, :], [Rwtab], [RW])
        if debug and b == 0:
            stores.append(DMA("sp", dbg["dbg_w"], Wtab, [RW], []))
        tmps = attn_temps()
        attention(0,
                  lambda h, q_, n: QA[:, h // 2, q_:q_ + n],
                  lambda h, j: KA[:, h, j * 128:(j + 1) * 128],
                  lambda h, j: VA[:, j, h, :], [RQA], [RKA], RVA, 0.125, Wtab, RW, *tmps)
        S.barrier()
        A.top = M1

        QT = A.alloc([8, S_LEN], BF16)
        KT = A.alloc([8, S_LEN], BF16)
        VB = A.alloc([16, 8, 65], BF16)
        RQT, RKT, RVB = S.R("QT"), S.R("KT"), S.R("VB")
        RQTr, RKTr = S.R("QTr"), S.R("KTr")
        w_uq = A.alloc([3, 1024], BF16)
        w_ukv = A.alloc([2, 1024], BF16)
        Rwu = S.R("wu")
        t1 = A.alloc([512], F32)
        t2 = A.alloc([512], F32)
        Rt1, Rt2 = S.R("t1b"), S.R("t2b")
        DMA("pool", w_uq, d_wuq, [], [Rwu])
        DMA("pool", w_ukv, d_wukv, [], [Rwu])
        S.op("pool", lambda e: e.memset(VB[:, :, :, 64:65], 1.0), [], [RVB])
        S.op("pool", lambda e: e.memset(QT[96:128], 0.0), [], [RQTr])
        S.op("pool", lambda e: e.memset(KT[96:128], 0.0), [], [RKTr])
        for hh in range(8):
            DMA("sp", KT[64:96, hh, :], KR[64:96, :], [Rkr], [RKTr])
        RQ = A.alloc([2, S_LEN], BF16)
        RRQ = S.R("RQ")
        for t in range(4):
            ts = slice(t * 512, (t + 1) * 512)
            for g in range(2):
                bqa, Rbqa = gen_bank()
                for c3 in range(3):
                    MM(bqa[:, :], w_uq[:, c3, 512 + 128 * g:512 + 128 * (g + 1)], cqnT[:, c3, ts], c3 == 0, c3 == 2,
                       [Rwu, Rlat], [Rbqa])
                bqb, Rbqb = gen_bank()
                for c3 in range(3):
                    MM(bqb[:, :], w_uq[:, c3, 768 + 128 * g:768 + 128 * (g + 1)], cqnT[:, c3, ts], c3 == 0, c3 == 2,
                       [Rwu, Rlat], [Rbqb])
                TT("dve", t1, bqa[:, :], rope[:, 0, ts], ALU.mult, [Rbqa, Rrope], [Rt1])
                TT("dve", t2, bqb[:, :], rope[:, 1, ts], ALU.mult, [Rbqb, Rrope], [Rt2])
                TT("pool", RQ[:, g, ts], t1, t2, ALU.add, [Rt1, Rt2], [RRQ])
                for hq in range(4):
                    h = 4 * g + hq
                    DMA("sp", QT[64:96, h, ts], RQ[32 * hq:32 * hq + 32, g, ts], [RRQ], [RQTr])
            for h in range(8):
                bq, Rbq = gen_bank()
                for c3 in range(3):
                    MM(bq[0:64, :], w_uq[:, c3, 64 * h:64 * h + 64], cqnT[:, c3, ts], c3 == 0, c3 == 2, [Rwu, Rlat], [Rbq])
                CP("act", QT[0:64, h, ts], bq[0:64, :], [Rbq], [RQT])
                bk, Rk = gen_bank()
                for c2 in range(2):
                    MM(bk[0:64, :], w_ukv[:, c2, 64 * h:64 * h + 64], ckvnT[:, c2, ts], c2 == 0, c2 == 1, [Rwu, Rlat], [Rk])
                CP("act" if h % 2 else "dve", KT[0:64, h, ts], bk[0:64, :], [Rk], [RKT])
            for i in range(4):
                bk, Rk = gen_bank()
                for c2 in range(2):
                    MM(bk[:, :], ckvnT[:, c2, t * 512 + i * 128:t * 512 + (i + 1) * 128], w_ukv[:, c2, 512:1024],
                       c2 == 0, c2 == 1, [Rwu, Rlat], [Rk])
                CP("dve", VB[:, 4 * t + i, :, 0:64], bk[:, :].rearrange("p (h e) -> p h e", e=64), [Rk], [RVB])
        if debug and b == 0:
            stores.append(DMA("sp", dbg["dbg_qt"], QT, [RQT, RQTr], []))
            stores.append(DMA("sp", dbg["dbg_kt"], KT, [RKT, RKTr], []))
        tmps = attn_temps()
        attention(1,
                  lambda h, q_, n: QT[:, h, q_:q_ + n],
                  lambda h, j: KT[:, h, j * 128:(j + 1) * 128],
                  lambda h, j: VB[:, j, h, :], [RQT, RQTr], [RKT, RKTr], RVB, 96.0 ** -0.5, None, None, *tmps)
        if debug and b == 0:
            stores.append(DMA("sp", dbg["dbg_y"], yT, [RyT], []))
        S.barrier()
        A.top = M0

        w_out = A.alloc([8, 8, 128], BF16)
        RwoP = [S.R("w_out%d" % i) for i in range(8)]
        for i_ in range(8):
            DMA("pool", w_out[:, i_], d_wout[:, i_], [], [RwoP[i_]])
        x1T = A.alloc([8, 1024], F32)
        Rx1 = [S.R("x1_%d" % i) for i in range(2)]
        h2T = A.alloc([8, 1024], BF16)
        Rh2 = [S.R("h2_%d" % i) for i in range(2)]
        hid = A.alloc([22, 1024], BF16)
        Rhid = [S.R("hid%d" % i) for i in range(2)]
        NXR = 4
        xres = [A.alloc([512], F32) for _ in range(NXR)]
        Rxres = [S.R("xres%d" % i) for i in range(NXR)]
        sq = A.alloc([8, 512], BF16)
        Rsq = S.R("sq4")
        tmp = [A.alloc([512], F32) for _ in range(2)]
        Rtmp = [S.R("tmp0b"), S.R("tmp1b")]
        rl = A.alloc([512], F32)
        rstd = A.alloc([512], F32)
        Rrl, Rrstd = S.R("rl4"), S.R("rstd4")
        sg = [A.alloc([512], F32) for _ in range(2)]
        Rsg = [S.R("sg0"), S.R("sg1")]
        NSTG = 4
        stage = [A.alloc([512], F32) for _ in range(NSTG)]
        Rstage = [S.R("stage%d" % i) for i in range(NSTG)]
        NWFI, NWFO = 3, 2
        wfi = [A.alloc([8, 256], BF16) for _ in range(NWFI)]
        Rwfi = [S.R("wfi%d" % i) for i in range(NWFI)]
        wfo = [A.alloc([22, 128], BF16) for _ in range(NWFO)]
        Rwfo = [S.R("wfo%d" % i) for i in range(NWFO)]
        xr_i = 0
        TP4 = (sq, Rsq, rl, Rrl, rstd, Rrstd, tmp, Rtmp)

        def adv(g, n):
            if g is not None:
                for _ in range(n):
                    next(g, None)

        def fin_gen(sc, t):
            tok = slice(sc * 1024 + t * 512, sc * 1024 + (t + 1) * 512)

            def fin_out(kc, tok=tok):
                st_, Rst_ = stage[kc % NSTG], Rstage[kc % NSTG]
                return st_, Rst_, (lambda: stores.append(DMA("sp", d_out[b, kc, :, tok], st_, [Rst_], [])))
            return norm_gen(x1T[:, :, t * 512:(t + 1) * 512], Rx1[t], 2, fin_out, None, None, TP4)

        def n2_gen(t):
            return norm_gen(x1T[:, :, t * 512:(t + 1) * 512], Rx1[t], 1,
                            (lambda kc: h2T[:, kc, t * 512:(t + 1) * 512]), Rh2[t], 3, TP4)

        pend = None
        for sc in range(2):
            for t in range(2):
                side = pend if t == 0 else n2_gen(0)
                pend = None
                for dc in range(8):
                    tok = slice(sc * 1024 + t * 512, sc * 1024 + (t + 1) * 512)
                    xr, Rxr = xres[xr_i % NXR], Rxres[xr_i % NXR]
                    xr_i += 1
                    DMA("sp", xr, d_x[b, dc, :, tok], [], [Rxr])
                    bk, Rk = gen_bank()
                    for kc in range(8):
                        MM(bk[:, :], w_out[:, dc, kc, :], yT[:, kc, tok], kc == 0, kc == 7, [RwoP[dc], RyT], [Rk])
                    STT(x1T[:, dc, t * 512:(t + 1) * 512], bk[:, :], modT[:, 16 + dc, b:b + 1], xr, ALU.mult, ALU.add,
                        [Rk, Rmod, Rxr], [Rx1[t]])
                    adv(side, 2)
                if side is not None:
                    run_all(side)
            if debug and b == 0 and sc == 0:
                stores.append(DMA("sp", dbg["dbg_x1"], x1T, Rx1, []))
            side = n2_gen(1)
            order = [(s_, 0) for s_ in range(3)] + [(s_, 1) for s_ in range(3)] + \
                    [(s_, t_) for s_ in range(3, 22) for t_ in range(2)]
            loaded = set()
            for (s_, t) in order:
                wf, Rwf = wfi[s_ % NWFI], Rwfi[s_ % NWFI]
                if s_ not in loaded:
                    loaded.add(s_)
                    DMA("pool", wf, d_wffi[:, s_], [], [Rwf])
                if (s_, t) == (0, 1):
                    run_all(side)
                    side = None
                tl = slice(t * 512, (t + 1) * 512)
                bg, Rbg = gen_bank()
                for kc in range(8):
                    MM(bg[:, :], wf[:, kc, 0:128], h2T[:, kc, tl], kc == 0, kc == 7, [Rwf, Rh2[t]], [Rbg])
                bu, Rbu = gen_bank()
                for kc in range(8):
                    MM(bu[:, :], wf[:, kc, 128:256], h2T[:, kc, tl], kc == 0, kc == 7, [Rwf, Rh2[t]], [Rbu])
                g_, Rg_ = sg[(2 * s_ + t) % 2], Rsg[(2 * s_ + t) % 2]
                ACT(g_, bg[:, :], AF.Silu, [Rbg], [Rg_])
                TT("dve", hid[:, s_, tl], g_, bu[:, :], ALU.mult, [Rg_, Rbu], [Rhid[t]])
                adv(side, 4)
            ffo_order = [(dc_, t_) for dc_ in range(6) for t_ in range(2)] + [(6, 0), (7, 0), (6, 1), (7, 1)]
            loaded_o = set()
            side = None
            for (dc, t) in ffo_order:
                wf, Rwf = wfo[dc % NWFO], Rwfo[dc % NWFO]
                if dc not in loaded_o:
                    loaded_o.add(dc)
                    DMA("pool", wf, d_wffo[:, dc], [], [Rwf])
                tl = slice(t * 512, (t + 1) * 512)
                bk, Rk = gen_bank()
                for s_ in range(22):
                    MM(bk[:, :], wf[:, s_, :], hid[:, s_, tl], s_ == 0, s_ == 21, [Rwf, Rhid[t]], [Rk])
                STT(x1T[:, dc, tl], bk[:, :], modT[:, 40 + dc, b:b + 1], x1T[:, dc, tl], ALU.mult, ALU.add,
                    [Rk, Rmod, Rx1[t]], [Rx1[t]])
                if (dc, t) == (7, 0):
                    side = fin_gen(sc, 0)
                    adv(side, 1)
                elif side is not None:
                    adv(side, 2 if (dc, t) == (6, 1) else 4)
            run_all(side)
            pend = fin_gen(sc, 1)
        run_all(pend)
        S.barrier()
        A.top = M0

    print("arena peak", A.peak, "ops", {e: len(S.ops[e]) for e in ENGS})
    S.emit(final_waits=stores)
    return nc


def _t5_bucket(dist):
    n_buckets, max_distance = 32, 2048
    max_exact = n_buckets // 2
    d = np.maximum(dist, 1).astype(np.float64)
    large = max_exact + (np.log(d / max_exact) / np.log(max_distance / max_exact) * (n_buckets - max_exact)).astype(np.int64)
    large = np.minimum(large, n_buckets - 1)
    return np.where(dist < max_exact, dist, large).astype(np.int32)


def _host_consts():
    ident = np.eye(128, dtype=np.float32)
    ones = np.ones((128, 128), np.float32)
    tri = (np.arange(128)[None, :] >= np.arange(128)[:, None]).astype(np.float32)
    consts = np.concatenate([ident, ones, tri, ident[::-1]], axis=1)
    delta = np.arange(S_LEN)
    mult = (delta <= 128).astype(np.float32) + ((delta % 4 == 0) & (delta <= 512)) + ((delta % 16 == 0) & (delta <= 2048))
    onehot = np.zeros((32, S_LEN), np.float32)
    onehot[_t5_bucket(delta), delta] = mult
    half = 16
    inv = (np.float32(10000.0) ** (-np.arange(half, dtype=np.float32) / np.float32(half))).astype(np.float32)
    ang = (np.arange(S_LEN, dtype=np.float32)[:, None] * inv[None, :]).astype(np.float32)
    cos, sin = np.cos(ang).astype(np.float32).T, np.sin(ang).astype(np.float32).T
    rope = np.zeros((128, 2, S_LEN), np.float32)
    for g4 in range(4):
        rope[32 * g4:32 * g4 + 16, 0], rope[32 * g4 + 16:32 * g4 + 32, 0] = cos, cos
        rope[32 * g4:32 * g4 + 16, 1], rope[32 * g4 + 16:32 * g4 + 32, 1] = -sin, sin
    return consts, onehot, rope


def _fm(v, n):
    return np.ascontiguousarray(np.asarray(v, np.float32).reshape(n, 128).T)


def _prep_inputs(x, c, w_ada, b_ada, g_norm1, w_in, g_cq, w_uq, g_ckv, w_ukv, rel_bias,
                 g_out_a, g_out_b, w_out, g_norm2, w_ffn_in, w_ffn_out, g_final):
    f = lambda a: np.asarray(a, np.float32)
    x, c = f(x), f(c)
    consts, onehot, rope = _host_consts()
    shared = {}
    shared["w_ada"] = np.ascontiguousarray(f(w_ada)[0].reshape(8, 128, 6, 1024).transpose(1, 2, 0, 3))
    shared["b_adaT"] = _fm(f(b_ada)[0], 48)
    gv = np.zeros((128, 40), np.float32)
    gv[:, 0:8] = _fm(f(g_norm1)[0], 8)
    gv[:, 8:16] = _fm(f(g_norm2)[0], 8)
    gv[:, 16:24] = _fm(f(g_final), 8)
    gv[:, 24:28] = _fm(f(g_out_a)[0], 4)
    gv[:, 28:32] = _fm(f(g_out_b)[0], 4)
    gv[:, 32:35] = _fm(f(g_cq)[0], 3)
    gv[:, 35:37] = _fm(f(g_ckv)[0], 2)
    shared["gvec"] = gv
    wi = f(w_in)[0]
    kr = wi[:, 2176:2208]
    wi2 = np.concatenate([wi, kr[:, 16:32], kr[:, 0:16]], axis=1)
    shared["w_in"] = np.ascontiguousarray(wi2.reshape(8, 128, 2240).transpose(1, 0, 2))
    wq = f(w_uq)[0].reshape(384, 8, 96)
    nope = wq[:, :, :64].reshape(384, 512)
    ropec = wq[:, :, 64:].reshape(384, 256)
    ropesw = np.concatenate([wq[:, :, 80:96], wq[:, :, 64:80]], axis=2).reshape(384, 256)
    wq2 = np.concatenate([nope, ropec, ropesw], axis=1)
    shared["w_uq"] = np.ascontiguousarray(wq2.reshape(3, 128, 1024).transpose(1, 0, 2))
    wk = f(w_ukv)[0].reshape(256, 8, 2, 64)
    wk2 = np.concatenate([wk[:, :, 0].reshape(256, 512), wk[:, :, 1].reshape(256, 512)], axis=1)
    shared["w_ukv"] = np.ascontiguousarray(wk2.reshape(2, 128, 1024).transpose(1, 0, 2))
    shared["rel_bias"] = f(rel_bias)
    shared["w_out"] = np.ascontiguousarray(f(w_out)[0].reshape(8, 128, 8, 128).transpose(1, 2, 0, 3))
    wfi = f(w_ffn_in)[0]
    g_, u_ = wfi[:, :DFF].reshape(8, 128, 22, 128), wfi[:, DFF:].reshape(8, 128, 22, 128)
    gu = np.concatenate([g_, u_], axis=3)
    shared["w_ffn_in"] = np.ascontiguousarray(gu.transpose(1, 2, 0, 3))
    wfo = f(w_ffn_out)[0].reshape(22, 128, 8, 128)
    shared["w_ffn_out"] = np.ascontiguousarray(wfo.transpose(1, 2, 0, 3))
    shared["consts"] = consts
    shared["onehot"] = onehot
    shared["rope"] = rope
    in_maps = []
    for core in range(NCORES):
        xb = x[2 * core:2 * core + 2]
        m = dict(shared)
        m["xT"] = np.ascontiguousarray(xb.transpose(0, 2, 1).reshape(2, 8, 128, S_LEN))
        cb = c[2 * core:2 * core + 2]
        m["cT"] = np.ascontiguousarray(cb.reshape(2, 8, 128).transpose(2, 1, 0))
        in_maps.append(m)
    return in_maps


_NC_CACHE = {}


def kernel(**inputs):
    in_maps = _prep_inputs(**inputs)
    if "nc" not in _NC_CACHE:
        _NC_CACHE["nc"] = build(debug=False)
    res = run_bass_kernel_spmd(_NC_CACHE["nc"], in_maps, core_ids=list(range(NCORES)))
    out = np.empty((16, S_LEN, D), np.float32)
    for core in range(NCORES):
        o = np.asarray(res.results[core]["outT"], np.float32).reshape(2, D, S_LEN)
        out[2 * core:2 * core + 2] = o.transpose(0, 2, 1)
    return out
```

```python
import numpy as np
import concourse.bass as bass
import concourse.mybir as mybir
from concourse.bass_utils import run_bass_kernel_spmd
from contextlib import ExitStack

F32 = mybir.dt.float32
BF16 = mybir.dt.bfloat16
AF = mybir.ActivationFunctionType
ALU = mybir.AluOpType

ENGS = ("pe", "act", "dve", "pool", "sp")
S_LEN = 2048
D = 1024
DFF = 2816
EPS = 1e-6
NCORES = 8


class Region:
    __slots__ = ("name", "w", "r", "excl")

    def __init__(self, name, excl=False):
        self.name = name
        self.w = None
        self.r = []
        self.excl = excl


class Sched:
    def __init__(self, nc, n_dma_sems=48):
        self.nc = nc
        self.ops = {e: [] for e in ENGS}
        self.count = {e: 0 for e in ENGS}
        self.n_dma = n_dma_sems
        self.dma_use = [0] * n_dma_sems
        self.n_sw = 16
        self.dma_next = {"sw": 0, "hw": 0}
        self.same_engine_wait = {"pe": False, "act": True, "dve": True, "pool": True, "sp": False}
        self.pending = {e: set() for e in ENGS}

    def R(self, name):
        return Region(name)

    def barrier(self):
        toks = set()
        for e in ENGS:
            if self.count[e] > 0:
                toks.add((e, self.count[e]))
        for i in range(self.n_dma):
            if self.dma_use[i] > 0:
                toks.add((("d", i), 16 * self.dma_use[i]))
        for e in ENGS:
            self.pending[e] |= toks

    def op(self, eng, fn, reads=(), writes=(), dma=False):
        deps = set()
        if self.pending[eng]:
            deps |= self.pending[eng]
            self.pending[eng] = set()
        def other(t):
            return t[0] != eng or eng == "pool"
        for r in reads:
            if r.w is not None:
                deps.add(r.w)
            if r.excl:
                deps.update(t for t in r.r if other(t))
        for w in writes:
            if w.w is not None and other(w.w):
                deps.add(w.w)
            deps.update(t for t in w.r if other(t))
        if dma:
            if eng == "pool":
                i = self.dma_next["sw"]
                self.dma_next["sw"] = (i + 1) % self.n_sw
            else:
                i = self.n_sw + self.dma_next["hw"]
                self.dma_next["hw"] = (self.dma_next["hw"] + 1) % (self.n_dma - self.n_sw)
            if self.dma_use[i] > 0:
                deps.add((("d", i), 16 * self.dma_use[i]))
            self.dma_use[i] += 1
            tok = (("d", i), 16 * self.dma_use[i])
        else:
            self.count[eng] += 1
            tok = (eng, self.count[eng])
        for r in reads:
            r.r.append(tok)
        for w in writes:
            w.w = tok
            w.r = []
        self.ops[eng].append((fn, deps, tok, dma))
        return tok

    def emit(self, final_waits=()):
        nc = self.nc
        with ExitStack() as st:
            sems = {}
            for e in ENGS:
                sems[e] = st.enter_context(nc.semaphore("s_" + e))
            for i in range(self.n_dma):
                sems[("d", i)] = st.enter_context(nc.semaphore("sd%d" % i))
            block = st.enter_context(nc.Block())

            def run(ename, eng):
                seen = {}
                for fn, deps, tok, dma in self.ops[ename]:
                    need = {}
                    for (s, v) in deps:
                        if s == ename and not self.same_engine_wait[ename]:
                            continue
                        if seen.get(s, 0) >= v:
                            continue
                        if need.get(s, 0) < v:
                            need[s] = v
                    for s, v in need.items():
                        eng.wait_ge(sems[s], v)
                        seen[s] = v
                    ins = fn(eng)
                    if dma:
                        ins.then_inc(sems[tok[0]], 16)
                    else:
                        ins.then_inc(sems[ename], 1)
                if ename == "sp":
                    for (s, v) in final_waits:
                        eng.wait_ge(sems[s], v)

            @block.tensor
            def _(eng):
                run("pe", eng)

            @block.scalar
            def _(eng):
                run("act", eng)

            @block.vector
            def _(eng):
                run("dve", eng)

            @block.gpsimd
            def _(eng):
                run("pool", eng)

            @block.sync
            def _(eng):
                run("sp", eng)


class Arena:
    def __init__(self, nc, total_bytes):
        self.t = nc.alloc_sbuf_tensor("arena", [128, total_bytes // 4], F32)
        self.top = 0
        self.total = total_bytes
        self.peak = 0

    def alloc(self, shape, dtype, at=None):
        n = int(np.prod(shape))
        bpe = 4 if dtype == F32 else 2
        nbytes = (n * bpe + 63) // 64 * 64
        if at is None:
            off = self.top
            self.top += nbytes
            self.peak = max(self.peak, self.top)
            assert self.top <= self.total, ("SBUF arena overflow", self.top, self.total)
        else:
            off = at
        self.last_off = off
        w0 = off // 4
        if dtype == F32:
            ap = self.t[:, w0:w0 + n]
        else:
            ap = self.t[:, w0:w0 + n // 2].bitcast(BF16)
        if len(shape) == 2:
            ap = ap.rearrange("p (a b) -> p a b", a=shape[0])
        elif len(shape) == 3:
            ap = ap.rearrange("p (a b c) -> p a b c", a=shape[0], b=shape[1])
        return ap


def build(debug=False):
    nc = bass.Bass("TRN2", target_bir_lowering=False)

    def din(name, shape, dt=F32):
        return nc.dram_tensor(name, list(shape), dt, kind="ExternalInput").ap()

    d_x = din("xT", [2, 8, 128, S_LEN])
    d_c = din("cT", [128, 8, 2])
    d_wada = din("w_ada", [128, 6, 8, 1024])
    d_bada = din("b_adaT", [128, 48])
    d_gvec = din("gvec", [128, 40])
    d_win = din("w_in", [128, 8, 2240])
    d_wuq = din("w_uq", [128, 3, 1024])
    d_wukv = din("w_ukv", [128, 2, 1024])
    d_relb = din("rel_bias", [32, 8])
    d_wout = din("w_out", [128, 8, 8, 128])
    d_wffi = din("w_ffn_in", [128, 22, 8, 256])
    d_wffo = din("w_ffn_out", [128, 8, 22, 128])
    d_const = din("consts", [128, 512])
    d_onehot = din("onehot", [32, S_LEN])
    d_rope = din("rope", [128, 2, S_LEN])
    d_out = nc.dram_tensor("outT", [2, 8, 128, S_LEN], F32, kind="ExternalOutput").ap()
    WP = 2176
    d_wpad_h = nc.dram_tensor("wpad", [8, WP], BF16)
    d_wpad = d_wpad_h.ap()
    d_wtab = nc.dram_tensor("wtab", [128, 8, S_LEN], BF16).ap()
    dbg = {}
    if debug:
        for nm, shp, dt in [("dbg_h", [128, 8, 512], BF16), ("dbg_qa", [128, 4, S_LEN], BF16),
                            ("dbg_va", [128, 16, 8, 65], BF16), ("dbg_cqn", [128, 3, S_LEN], BF16),
                            ("dbg_kr", [128, S_LEN], BF16), ("dbg_y", [128, 8, S_LEN], BF16),
                            ("dbg_x1", [128, 8, 1024], F32), ("dbg_mod", [128, 48, 2], F32),
                            ("dbg_w", [128, 8, S_LEN], BF16), ("dbg_qt", [128, 8, S_LEN], BF16),
                            ("dbg_kt", [128, 8, S_LEN], BF16)]:
            dbg[nm] = nc.dram_tensor(nm, shp, dt, kind="ExternalOutput").ap()

    S = Sched(nc)
    A = Arena(nc, 207 * 1024)
    banks = [nc.alloc_psum_tensor("psb%d" % i, [128, 512], F32) for i in range(7)]
    bankT = nc.alloc_psum_tensor("psT", [128, 1024], BF16)
    Rb = [Region("bank%d" % i, excl=True) for i in range(7)]
    RbT = Region("bankT", excl=True)
    stores = []

    def MM(out, lhsT, rhs, start, stop, reads, writes):
        return S.op("pe", lambda e: e.matmul(out, lhsT, rhs, start=start, stop=stop), reads, writes)

    def ACT(out, in_, func, reads, writes, bias=None, scale=None, accum_out=None):
        kw = {}
        if bias is not None:
            kw["bias"] = bias
        if scale is not None:
            kw["scale"] = scale
        if accum_out is not None:
            kw["accum_out"] = accum_out
        return S.op("act", lambda e: e.activation(out=out, in_=in_, func=func, **kw), reads, writes)

    def TT(eng, out, in0, in1, op, reads, writes):
        return S.op(eng, lambda e: e.tensor_tensor(out=out, in0=in0, in1=in1, op=op), reads, writes)

    def TS(eng, out, in0, s1, s2, op0, op1, reads, writes):
        return S.op(eng, lambda e: e.tensor_scalar(out=out, in0=in0, scalar1=s1, scalar2=s2, op0=op0, op1=op1),
                    reads, writes)

    def STT(out, in0, scalar, in1, op0, op1, reads, writes, accum_out=None):
        kw = {}
        if accum_out is not None:
            kw["accum_out"] = accum_out
        return S.op("dve", lambda e: e.scalar_tensor_tensor(out=out, in0=in0, scalar=scalar, in1=in1,
                                                            op0=op0, op1=op1, **kw), reads, writes)

    def CP(eng, out, in_, reads, writes):
        if eng == "act":
            return S.op("act", lambda e: e.copy(out=out, in_=in_), reads, writes)
        return S.op(eng, lambda e: e.tensor_copy(out=out, in_=in_), reads, writes)

    def RECIP(out, in_, reads, writes):
        return S.op("dve", lambda e: e.reciprocal(out=out, in_=in_), reads, writes)

    def TR(out, in_, reads, writes):
        return S.op("pe", lambda e: e.transpose(out, in_, ident), reads, writes)

    def DMA(eng, out, in_, reads, writes):
        return S.op(eng, lambda e: e.dma_start(out=out, in_=in_), reads, writes, dma=True)

    gen_i = [0]

    def gen_bank():
        i = gen_i[0] % 6
        gen_i[0] += 1
        return banks[i], Rb[i]

    cst = A.alloc([512], BF16)
    ident, ones_bf, tri, Jx = cst[:, 0:128], cst[:, 128:256], cst[:, 256:384], cst[:, 384:512]
    Rcst = S.R("cst")
    modT = A.alloc([48, 2], F32)
    Rmod = S.R("modT")
    gvec = A.alloc([40], F32)
    Rgv = S.R("gvec")
    geff = A.alloc([2, 8, 2], F32)
    zero1 = A.alloc([2], F32)
    epsb = A.alloc([2], F32)
    condT = A.alloc([8, 2], F32)
    badaT = A.alloc([48], F32)
    Rsmall = S.R("small")
    id32 = A.alloc([128], F32)
    condB = A.alloc([8, 2], F32)
    m32 = A.alloc([1024], F32)
    id2 = A.alloc([2], F32)
    yT = A.alloc([8, S_LEN], BF16)
    yT_off = A.last_off
    RyT = S.R("yT")
    M0 = A.top

    DMA("pool", cst, d_const, [], [Rcst])
    DMA("sp", id32, d_const[:, 0:128], [], [Rcst])
    DMA("sp", gvec, d_gvec, [], [Rgv])
    DMA("sp", badaT, d_bada, [], [Rsmall])
    DMA("sp", condT, d_c, [], [Rsmall])
    S.op("dve", lambda e: e.memset(zero1, 0.0), [], [Rsmall])
    S.op("dve", lambda e: e.memset(epsb, EPS), [], [Rsmall])
    epsap = epsb[:, 0:1]
    zap = zero1[:, 0:1]

    WIN_PIECES = [(0, 512), (512, 1024), (1024, 1536), (1536, 2240)]
    WIN_OFF = M0 + 8192 + 12288 + 8192 + 4096 + 16384 + 32768 + 16640
    w_in0 = A.alloc([8, 2240], BF16, at=WIN_OFF)
    Rwin0 = [S.R("w_in_p%d" % i) for i in range(4)]
    for i_, (c0_, c1_) in enumerate(WIN_PIECES):
        DMA("pool", w_in0[:, :, c0_:c1_], d_win[:, :, c0_:c1_], [], [Rwin0[i_]])
    wa = [A.alloc([8, 1024], F32) for _ in range(2)]
    Rwa = [S.R("wa0"), S.R("wa1")]
    RcondS = S.R("condS")
    Rm32 = S.R("m32")
    assert A.top <= WIN_OFF, (A.top, WIN_OFF)
    A.top = WIN_OFF + 8 * 2240 * 2
    relb = A.alloc([8], F32)
    oneh = A.alloc([S_LEN], F32)
    wv = A.alloc([WP], BF16)
    wtp = [A.alloc([S_LEN], BF16) for _ in range(2)]
    wts = [A.alloc([S_LEN], BF16) for _ in range(2)]
    Rtz = S.R("tz")
    Rwpad = S.R("wpad")
    Rwtab = S.R("wtab")
    Rwtp = [S.R("wtp0"), S.R("wtp1")]
    Rwts = [S.R("wts0"), S.R("wts1")]
    DMA("sp", id2[0:2], d_const[0:2, 0:2], [], [Rm32])
    DMA("sp", relb[0:32], d_relb, [], [Rtz])
    DMA("sp", oneh[0:32], d_onehot, [], [Rtz])
    ACT(condB, condT, AF.Silu, [Rsmall], [RcondS])

    def mod_finish(j, halves):
        for half, (bk, Rk) in enumerate(halves):
            CP("act" if half else "dve", m32[0:2, half * 512:(half + 1) * 512], bk[0:2, :], [Rk], [Rm32])
        mod_tail(j)

    def mod_slot(j):
        w, Rw = wa[j % 2], Rwa[j % 2]
        DMA("sp", w, d_wada[:, j], [], [Rw])
        for half in range(2):
            bk, Rk = gen_bank()
            for kc in range(8):
                MM(bk[0:2, :], condB[:, kc, :], w[:, kc, half * 512:(half + 1) * 512], kc == 0, kc == 7,
                   [Rw, RcondS], [Rk])
            CP("act" if half else "dve", m32[0:2, half * 512:(half + 1) * 512], bk[0:2, :], [Rk], [Rm32])
        mod_tail(j)

    def mod_tail(j):
        bkT_, RkT_ = gen_bank()
        for f in range(8):
            S.op("pe", (lambda f=f, bkT_=bkT_: (lambda e: e.transpose(bkT_[:, 2 * f:2 * f + 2],
                                                                    m32[0:2, f * 128:(f + 1) * 128], id2[0:2, 0:2])))(),
                 [Rm32], [RkT_])
        TT("dve", modT[:, j * 8:(j + 1) * 8, :], bkT_[:, 0:16].rearrange("p (f b) -> p f b", b=2),
           badaT[:, j * 8:(j + 1) * 8].unsqueeze(2).to_broadcast([128, 8, 2]), ALU.add, [RkT_, Rsmall], [Rmod])

    def toe_setup():
        ACT(relb[0:32], relb[0:32], AF.Exp, [Rtz], [Rtz])
        S.op("dve", lambda e: e.memset(wv[0:8, 0:128], 0.0), [], [Rtz])
        for q in range(4):
            bk, Rk = gen_bank()
            MM(bk[0:8, :], relb[0:32, :], oneh[0:32, q * 512:(q + 1) * 512], True, True, [Rtz], [Rk])
            CP("dve", wv[0:8, 127 + q * 512:127 + (q + 1) * 512], bk[0:8, :], [Rk], [Rtz])
        DMA("act", d_wpad[:, 0:2175], wv[0:8, 0:2175], [Rtz], [Rwpad])

    def toe_load(hh):
        src = bass.AP(tensor=d_wpad_h, offset=hh * WP, ap=[[1, 128], [1, S_LEN]])
        DMA("act", wtp[hh % 2], src, [Rwpad], [Rwtp[hh % 2]])

    def toe_head(hh):
        for q in range(4):
            bk, Rk = gen_bank()
            MM(bk[:, :], Jx, wtp[hh % 2][:, q * 512:(q + 1) * 512], True, True, [Rcst, Rwtp[hh % 2]], [Rk])
            CP("dve" if q % 2 else "act", wts[hh % 2][:, q * 512:(q + 1) * 512], bk[:, :], [Rk], [Rwts[hh % 2]])
        if hh + 2 < 8:
            toe_load(hh + 2)
        DMA("act", d_wtab[:, hh, :], wts[hh % 2], [Rwts[hh % 2]], [Rwtab])

    def make_geff(n, gofs, scslot):
        TS("dve", geff[:, n], modT[:, scslot * 8:(scslot + 1) * 8, :], 1.0, None, ALU.add, ALU.bypass,
           [Rmod], [Rsmall])
        TT("dve", geff[:, n], geff[:, n], gvec[:, gofs:gofs + 8].unsqueeze(2).to_broadcast([128, 8, 2]), ALU.mult,
           [Rsmall, Rgv], [Rsmall])

    toe_setup()
    toe_load(0)
    toe_load(1)
    toe_head(0)
    mod_slot(0)
    toe_head(1)
    mod_slot(1)
    for i_ in range(2, 8):
        toe_head(i_)
    make_geff(0, 0, 1)
    S.barrier()
    A.top = M0

    def fm_rstd(sq_list, nfeat, sq_reads):
        bk, Rk = gen_bank()
        n = len(sq_list)
        for i, sq in enumerate(sq_list):
            MM(bk[:, :], ones_bf, sq, i == 0, i == n - 1, [Rcst] + sq_reads, [Rk])
        return bk[:, :], Rk

    for b in range(2):
        rope = A.alloc([2, S_LEN], BF16)
        Rrope = S.R("rope")
        cqnT = A.alloc([3, S_LEN], BF16)
        ckvnT = A.alloc([2, S_LEN], BF16)
        KR = A.alloc([S_LEN], BF16)
        Rlat = S.R("lat")
        Rkr = S.R("kr")
        M1 = A.top
        QA = A.alloc([4, S_LEN], BF16)
        KA = A.alloc([8, S_LEN], BF16)
        VA = A.alloc([16, 8, 65], BF16)
        RQA, RKA, RVA = S.R("QA"), S.R("KA"), S.R("VA")
        M2 = A.top
        w_in = A.alloc([8, 2240], BF16)
        assert A.last_off == WIN_OFF, (A.last_off, WIN_OFF)
        RwinP = Rwin0 if b == 0 else [S.R("w_in_b1p%d" % i) for i in range(4)]
        xc = A.alloc([8, 512], F32, at=yT_off)
        Rxc = S.R("xc")
        sq = A.alloc([8, 512], BF16, at=yT_off + 16384)
        Rsq = S.R("sq")
        hT = A.alloc([8, 512], BF16, at=yT_off + 24576)
        RhT = S.R("hT")
        tmp = [A.alloc([512], F32) for _ in range(2)]
        Rtmp = [S.R("tmp0"), S.R("tmp1")]
        rl = A.alloc([512], F32)
        rstd = A.alloc([512], F32)
        Rrl, Rrstd = S.R("rl"), S.R("rstd")
        lat32 = A.alloc([3, 512], F32)
        sql = A.alloc([3, 512], BF16)
        Rlat32, Rsql = S.R("lat32"), S.R("sql")
        t1, t2 = tmp[0], tmp[1]
        Rt1, Rt2 = Rtmp[0], Rtmp[1]
        piece = A.alloc([1024], F32)
        Rpiece = S.R("piece")

        def mod_late_gen():
            bank7 = bankT[:, :].bitcast(F32)
            for j in range(2, 6):
                for kc in range(8):
                    DMA("sp", piece, d_wada[:, j, kc, :], [], [Rpiece])
                    MM(banks[6][0:2, :], condB[:, kc, :], piece[:, 0:512], kc == 0, kc == 7, [Rpiece, RcondS], [Rb[6]])
                    MM(bank7[0:2, :], condB[:, kc, :], piece[:, 512:1024], kc == 0, kc == 7, [Rpiece, RcondS], [RbT])
                    yield
                mod_finish(j, [(banks[6], Rb[6]), (bank7, RbT)])
                yield

        DMA("pool", rope, d_rope, [], [Rrope])
        if b > 0:
            for i_, (c0_, c1_) in enumerate(WIN_PIECES):
                DMA("pool", w_in[:, :, c0_:c1_], d_win[:, :, c0_:c1_], [], [RwinP[i_]])
        S.op("pool", lambda e: e.memset(VA[:, :, :, 64:65], 1.0), [], [RVA])
        for hh in range(8):
            lo = 64 if hh % 2 == 0 else 0
            S.op("pool", (lambda hh=hh, lo=lo: (lambda e: e.memset(KA[lo:lo + 64, hh, :], 0.0)))(), [], [RKA])

        def norm_gen(xin, Rxin, nrm, out_fn, Rout, shift_slot, T):
            sq_, Rsq_, rl_, Rrl_, rstd_, Rrstd_, tmp_, Rtmp_ = T
            ACT(sq_, xin, AF.Square, [Rxin], [Rsq_])
            yield
            bk, Rk = fm_rstd([sq_[:, kc, :] for kc in range(8)], D, [Rsq_])
            yield
            ACT(rl_, bk, AF.Ln, [Rk, Rsmall], [Rrl_], bias=epsap, scale=1.0 / D)
            ACT(rstd_, rl_, AF.Exp, [Rrl_], [Rrstd_], scale=-0.5)
            yield
            for kc in range(8):
                tm, Rtm = tmp_[kc % 2], Rtmp_[kc % 2]
                TT("dve", tm, xin[:, kc, :], rstd_, ALU.mult, [Rxin, Rrstd_], [Rtm])
                if nrm < 2:
                    sh = modT[:, shift_slot * 8 + kc, b:b + 1]
                    ACT(out_fn(kc), tm, AF.Identity, [Rtm, Rsmall, Rmod], [Rout], bias=sh,
                        scale=geff[:, nrm, kc, b:b + 1])
                else:
                    o_, Ro_, after = out_fn(kc)
                    ACT(o_, tm, AF.Identity, [Rtm, Rgv], [Ro_], scale=gvec[:, 16 + kc:17 + kc])
                    after()
                yield

        def run_all(g):
            for _ in g:
                pass

        def norm_chunk(xin, Rxin, nrm, out_fn, Rout, shift_slot):
            run_all(norm_gen(xin, Rxin, nrm, out_fn, Rout, shift_slot, (sq, Rsq, rl, Rrl, rstd, Rrstd, tmp, Rtmp)))

        hTs = [hT, A.alloc([8, 512], BF16)]
        RhTs = [RhT, S.R("hT1")]
        rlX = A.alloc([512], F32)
        rstdX = A.alloc([512], F32)
        tmpX = [A.alloc([512], F32) for _ in range(2)]
        TX = (sq, Rsq, rlX, S.R("rlX"), rstdX, S.R("rstdX"), tmpX, [S.R("tmpX0"), S.R("tmpX1")])

        def p1_norm(t):
            ts_ = slice(t * 512, (t + 1) * 512)
            DMA("sp", xc, d_x[b, :, :, ts_].rearrange("k p t -> p k t"), [], [Rxc])
            h_, Rh_ = hTs[t % 2], RhTs[t % 2]
            return norm_gen(xc, Rxc, 0, lambda kc: h_[:, kc, :], Rh_, 0, TX)

        def p1_groups(t, hT, RhT):
            ts = slice(t * 512, (t + 1) * 512)
            G = []

            def g_qk(gi, i):
                Rwin = RwinP[gi]
                dst, Rd, cofs = (QA, RQA, 0) if gi == 0 else (KA, RKA, 512)
                bk, Rk = gen_bank()
                for kc in range(8):
                    MM(bk[:, :], w_in[:, kc, cofs + i * 128:cofs + (i + 1) * 128], hT[:, kc, :], kc == 0, kc == 7,
                       [Rwin, RhT], [Rk])
                if gi == 0:
                    CP("act", dst[:, i, ts], bk[:, :], [Rk], [Rd])
                else:
                    CP("dve", KA[0:64, 2 * i, ts], bk[0:64, :], [Rk], [Rd])
                    CP("dve", KA[64:128, 2 * i + 1, ts], bk[64:128, :], [Rk], [Rd])

            def g_lat(i):
                Rwin = RwinP[3]
                cofs = 1536 + i * 128
                bk, Rk = gen_bank()
                for kc in range(8):
                    MM(bk[:, :], w_in[:, kc, cofs:cofs + 128], hT[:, kc, :], kc == 0, kc == 7, [Rwin, RhT], [Rk])
                CP("dve", lat32[:, i % 3, :], bk[:, :], [Rk], [Rlat32])
                ACT(sql[:, i % 3, :], bk[:, :], AF.Square, [Rk], [Rsql])

            def g_latnorm(lo, n, dst, gofs):
                bk, Rk = fm_rstd([sql[:, i, :] for i in range(n)], n * 128, [Rsql])
                ACT(rl, bk, AF.Ln, [Rk, Rsmall], [Rrl], bias=epsap, scale=1.0 / (n * 128))
                ACT(rstd, rl, AF.Exp, [Rrl], [Rrstd], scale=-0.5)
                for i in range(n):
                    tm, Rtm = tmp[i % 2], Rtmp[i % 2]
                    TT("dve", tm, lat32[:, i, :], rstd, ALU.mult, [Rlat32, Rrstd], [Rtm])
                    TS("pool", dst[:, i, ts], tm, gvec[:, gofs + i:gofs + i + 1], zap, ALU.mult, ALU.add,
                       [Rtm, Rgv, Rsmall], [Rlat])

            def g_kr():
                Rwin = RwinP[3]
                bkA, RkA = gen_bank()
                for kc in range(8):
                    MM(bkA[64:96, :], w_in[:, kc, 2176:2208], hT[:, kc, :], kc == 0, kc == 7, [Rwin, RhT], [RkA])
                bkB, RkB = gen_bank()
                for kc in range(8):
                    MM(bkB[64:96, :], w_in[:, kc, 2208:2240], hT[:, kc, :], kc == 0, kc == 7, [Rwin, RhT], [RkB])
                TT("dve", t1[64:96], bkA[64:96, :], rope[64:96, 0, ts], ALU.mult, [RkA, Rrope], [Rt1])
                TT("dve", t2[64:96], bkB[64:96, :], rope[64:96, 1, ts], ALU.mult, [RkB, Rrope], [Rt2])
                TT("pool", KR[64:96, ts], t1[64:96], t2[64:96], ALU.add, [Rt1, Rt2], [Rkr])

            def g_va(i):
                Rwin = RwinP[2]
                bk, Rk = gen_bank()
                for kc in range(8):
                    MM(bk[:, :], hT[:, kc, i * 128:(i + 1) * 128], w_in[:, kc, 1024:1536], kc == 0, kc == 7,
                       [Rwin, RhT], [Rk])
                CP("act", VA[:, 4 * t + i, :, 0:64], bk[:, :].rearrange("p (h e) -> p h e", e=64), [Rk], [RVA])

            for gi in range(2):
                for i in range(4):
                    G.append((lambda gi=gi, i=i: g_qk(gi, i)))
            for i in range(3):
                G.append((lambda i=i: g_lat(i)))
            G.append(lambda: g_latnorm(0, 3, cqnT, 32))
            for i in range(3, 5):
                G.append((lambda i=i: g_lat(i)))
            G.append(lambda: g_latnorm(3, 2, ckvnT, 35))
            G.append(g_kr)
            for i in range(4):
                G.append((lambda i=i: g_va(i)))
            return G

        run_all(p1_norm(0))
        if debug and b == 0:
            stores.append(DMA("sp", dbg["dbg_h"], hTs[0], [RhTs[0]], []))
        late = mod_late_gen() if b == 0 else None
        for t in range(4):
            nxt = p1_norm(t + 1) if t + 1 < 4 else None
            for gi_, g in enumerate(p1_groups(t, hTs[t % 2], RhTs[t % 2])):
                g()
                if nxt is not None:
                    next(nxt, None)
                if late is not None and gi_ % 2 == 1:
                    next(late, None)
            if nxt is not None:
                run_all(nxt)
        if late is not None:
            run_all(late)
            make_geff(1, 8, 4)
            if debug:
                stores.append(DMA("sp", dbg["dbg_mod"], modT, [Rmod], []))
        if debug and b == 0:
            stores.append(DMA("sp", dbg["dbg_qa"], QA, [RQA], []))
            stores.append(DMA("sp", dbg["dbg_va"], VA, [RVA], []))
            stores.append(DMA("sp", dbg["dbg_cqn"], cqnT, [Rlat], []))
            stores.append(DMA("sp", dbg["dbg_kr"], KR, [Rkr], []))
        S.barrier()
        A.top = M2

        def attention(grp, qT, kT, vaug, Rq, Rk_, Rv, scale, Wtab, RW, O, RO, PT, RPT, rec, Rrec, On, ROn, ss, Rss):
            NPT = len(PT)
            SB = [0, 1, 6]
            L = 3
            tiles = [(c, h, j) for c in range(4) for h in range(8) for j in range(4 * c + 4)]
            N = len(tiles)
            info = {}
            deferred = []

            def emit_S(idx):
                c, h, j = tiles[idx]
                q0 = max(4 * c, j)
                off = (q0 - 4 * c) * 128
                n = (4 * c + 4 - q0) * 128
                sbi = SB[idx % 3]
                sb, Rsb = banks[sbi], Rb[sbi]
                MM(sb[:, off:off + n], kT(h, j), qT(h, q0 * 128, n), True, True, list(Rq) + list(Rk_), [Rsb])
                pt, Rpt = PT[idx % NPT], RPT[idx % NPT]
                ACT(pt[:, off:off + n], sb[:, off:off + n], AF.Exp, [Rsb], [Rpt], scale=scale)
                if grp == 0:
                    d0 = (q0 - j) * 128
                    TT("dve", pt[:, off:off + n], pt[:, off:off + n], Wtab[:, h, d0:d0 + n], ALU.mult,
                       [Rpt, RW], [Rpt])
                elif j >= 4 * c:
                    TT("dve", pt[:, off:off + 128], pt[:, off:off + 128], tri, ALU.mult, [Rpt, Rcst], [Rpt])
                info[idx] = (pt, Rpt, q0)

            def stage_b(c, ii):
                def fn():
                    tile_ = 4 * c + ii
                    for m in range(4):
                        TR(bankT[:, m * 128:(m + 1) * 128], On[ii][:, m * 128:(m + 1) * 128], [ROn[ii], Rcst], [RbT])
                    for m in range(4):
                        TS("dve", yT[:, grp * 4 + m, tile_ * 128:(tile_ + 1) * 128], bankT[:, m * 128:(m + 1) * 128],
                           gvec[:, 24 + grp * 4 + m:25 + grp * 4 + m], None, ALU.mult, ALU.bypass, [RbT, Rgv], [RyT])
                return fn

            now = [0]

            def emit_PV(idx):
                c, h, j = tiles[idx]
                pt, Rpt, q0 = info.pop(idx)
                ch = c * 8 + h
                ab = 2 + (ch % 2)
                off = (q0 - 4 * c) * 128
                MM(banks[ab][0:65, off:512], vaug(h, j), pt[:, off:512], j == 0, j == 4 * c + 3, [Rpt, Rv], [Rb[ab]])
                if j == 4 * c + 3:
                    ot, Rot = OT[ch % 2], ROT[ch % 2]
                    CP("act", ot[0:65, :], banks[ab][0:65, :], [Rb[ab]], [Rot])

                    def fin(c=c, h=h, ch=ch, ot=ot, Rot=Rot):
                        Ob, ROb = O[c % 2], RO[c % 2]
                        xb = 4 + (ch % 2)
                        for ii in range(4):
                            S.op("pe", (lambda ii=ii: (lambda e: e.transpose(
                                banks[xb][:, ii * 65:(ii + 1) * 65], ot[0:65, ii * 128:(ii + 1) * 128], id32[0:65, 0:65])))(),
                                [Rot, Rcst], [Rb[xb]])
                        CP("dve", Ob[:, :, h, :], banks[xb][:, 0:260].rearrange("p (i e) -> p i e", e=65), [Rb[xb]], [ROb])
                        if h == 7:
                            for ii in range(4):
                                s_ = ss[:, ii, :]
                                r8, Rr8 = rec[ii], Rrec[ii]
                                of_, Rof_ = Of[ii % 2], ROf[ii % 2]
                                RECIP(r8, Ob[:, ii, :, 64], [ROb], [Rr8])
                                TT("dve", of_.rearrange("p (h e) -> p h e", e=64), Ob[:, ii, :, 0:64],
                                   r8.unsqueeze(2).to_broadcast([128, 8, 64]), ALU.mult, [ROb, Rr8], [Rof_])
                                STT(On[ii], of_, 1.0, of_, ALU.mult, ALU.mult, [Rof_], [ROn[ii], Rss[ii]],
                                    accum_out=s_[:, 0:1])
                                ACT(s_[:, 1:2], s_[:, 0:1], AF.Ln, [Rss[ii], Rsmall], [Rss[ii]], bias=epsap, scale=1.0 / 512)
                                ACT(s_[:, 2:3], s_[:, 1:2], AF.Exp, [Rss[ii]], [Rss[ii]], scale=-0.5)
                                TS("dve", On[ii], of_, s_[:, 2:3], None, ALU.mult, ALU.bypass, [Rof_, Rss[ii]], [ROn[ii]])
                                deferred.append((now[0] + 3 + 2 * ii, stage_b(c, ii)))
                            deferred.sort(key=lambda x_: x_[0])
                    deferred.append((idx + L + 2, fin))
                    deferred.sort(key=lambda x_: x_[0])

            for idx in range(N + L):
                now[0] = idx
                if idx < N:
                    emit_S(idx)
                if idx >= L:
                    emit_PV(idx - L)
                while deferred and deferred[0][0] <= idx:
                    deferred.pop(0)[1]()
            while deferred:
                now[0] += 1
                deferred.pop(0)[1]()

        Of, ROf = [None, None], [None, None]
        OT, ROT = [None, None], [None, None]

        def attn_temps():
            O = [A.alloc([4, 8, 65], F32) for _ in range(2)]
            PT = [A.alloc([512], BF16) for _ in range(5)]
            rec = [A.alloc([8], F32) for _ in range(8)]
            On = [A.alloc([512], BF16) for _ in range(4)]
            ss = A.alloc([4, 4], F32)
            Of[:] = [A.alloc([512], F32) for _ in range(2)]
            ROf[:] = [S.R("Of0"), S.R("Of1")]
            OT[:] = [A.alloc([512], F32) for _ in range(2)]
            ROT[:] = [S.R("OT0"), S.R("OT1")]
            return (O, [S.R("O0"), S.R("O1")], PT, [S.R("pt%d" % i) for i in range(5)], rec,
                    [S.R("rec%d" % i) for i in range(8)], On, [S.R("On%d" % i) for i in range(4)], ss,
                    [S.R("ss%d" % i) for i in range(4)])

        Wtab = A.alloc([8, S_LEN], BF16)
        RW = S.R("Wtab")
        for hh in range(8):
            DMA("sp", Wtab[:, hh, :], d_wtab[:, hh, :], [Rwtab], [RW])
        if debug and b == 0:
            stores.append(DMA("sp", dbg["dbg_w"], Wtab, [RW], []))
        tmps = attn_temps()
        attention(0,
                  lambda h, q_, n: QA[:, h // 2, q_:q_ + n],
                  lambda h, j: KA[:, h, j * 128:(j + 1) * 128],
                  lambda h, j: VA[:, j, h, :], [RQA], [RKA], RVA, 0.125, Wtab, RW, *tmps)
        S.barrier()
        A.top = M1

        QT = A.alloc([8, S_LEN], BF16)
        KT = A.alloc([8, S_LEN], BF16)
        VB = A.alloc([16, 8, 65], BF16)
        RQT, RKT, RVB = S.R("QT"), S.R("KT"), S.R("VB")
        RQTr, RKTr = S.R("QTr"), S.R("KTr")
        w_uq = A.alloc([3, 1024], BF16)
        w_ukv = A.alloc([2, 1024], BF16)
        Rwu = S.R("wu")
        t1 = A.alloc([512], F32)
        t2 = A.alloc([512], F32)
        Rt1, Rt2 = S.R("t1b"), S.R("t2b")
        DMA("pool", w_uq, d_wuq, [], [Rwu])
        DMA("pool", w_ukv, d_wukv, [], [Rwu])
        S.op("pool", lambda e: e.memset(VB[:, :, :, 64:65], 1.0), [], [RVB])
        S.op("pool", lambda e: e.memset(QT[96:128], 0.0), [], [RQTr])
        S.op("pool", lambda e: e.memset(KT[96:128], 0.0), [], [RKTr])
        for hh in range(8):
            DMA("sp", KT[64:96, hh, :], KR[64:96, :], [Rkr], [RKTr])
        RQ = A.alloc([2, S_LEN], BF16)
        RRQ = S.R("RQ")
        for t in range(4):
            ts = slice(t * 512, (t + 1) * 512)
            for g in range(2):
                bqa, Rbqa = gen_bank()
                for c3 in range(3):
                    MM(bqa[:, :], w_uq[:, c3, 512 + 128 * g:512 + 128 * (g + 1)], cqnT[:, c3, ts], c3 == 0, c3 == 2,
                       [Rwu, Rlat], [Rbqa])
                bqb, Rbqb = gen_bank()
                for c3 in range(3):
                    MM(bqb[:, :], w_uq[:, c3, 768 + 128 * g:768 + 128 * (g + 1)], cqnT[:, c3, ts], c3 == 0, c3 == 2,
                       [Rwu, Rlat], [Rbqb])
                TT("dve", t1, bqa[:, :], rope[:, 0, ts], ALU.mult, [Rbqa, Rrope], [Rt1])
                TT("dve", t2, bqb[:, :], rope[:, 1, ts], ALU.mult, [Rbqb, Rrope], [Rt2])
                TT("pool", RQ[:, g, ts], t1, t2, ALU.add, [Rt1, Rt2], [RRQ])
                for hq in range(4):
                    h = 4 * g + hq
                    DMA("sp", QT[64:96, h, ts], RQ[32 * hq:32 * hq + 32, g, ts], [RRQ], [RQTr])
            for h in range(8):
                bq, Rbq = gen_bank()
                for c3 in range(3):
                    MM(bq[0:64, :], w_uq[:, c3, 64 * h:64 * h + 64], cqnT[:, c3, ts], c3 == 0, c3 == 2, [Rwu, Rlat], [Rbq])
                CP("act", QT[0:64, h, ts], bq[0:64, :], [Rbq], [RQT])
                bk, Rk = gen_bank()
                for c2 in range(2):
                    MM(bk[0:64, :], w_ukv[:, c2, 64 * h:64 * h + 64], ckvnT[:, c2, ts], c2 == 0, c2 == 1, [Rwu, Rlat], [Rk])
                CP("act" if h % 2 else "dve", KT[0:64, h, ts], bk[0:64, :], [Rk], [RKT])
            for i in range(4):
                bk, Rk = gen_bank()
                for c2 in range(2):
                    MM(bk[:, :], ckvnT[:, c2, t * 512 + i * 128:t * 512 + (i + 1) * 128], w_ukv[:, c2, 512:1024],
                       c2 == 0, c2 == 1, [Rwu, Rlat], [Rk])
                CP("dve", VB[:, 4 * t + i, :, 0:64], bk[:, :].rearrange("p (h e) -> p h e", e=64), [Rk], [RVB])
        if debug and b == 0:
            stores.append(DMA("sp", dbg["dbg_qt"], QT, [RQT, RQTr], []))
            stores.append(DMA("sp", dbg["dbg_kt"], KT, [RKT, RKTr], []))
        tmps = attn_temps()
        attention(1,
                  lambda h, q_, n: QT[:, h, q_:q_ + n],
                  lambda h, j: KT[:, h, j * 128:(j + 1) * 128],
                  lambda h, j: VB[:, j, h, :], [RQT, RQTr], [RKT, RKTr], RVB, 96.0 ** -0.5, None, None, *tmps)
        if debug and b == 0:
            stores.append(DMA("sp", dbg["dbg_y"], yT, [RyT], []))
        S.barrier()
        A.top = M0

        w_out = A.alloc([8, 8, 128], BF16)
        RwoP = [S.R("w_out%d" % i) for i in range(8)]
        for i_ in range(8):
            DMA("pool", w_out[:, i_], d_wout[:, i_], [], [RwoP[i_]])
        x1T = A.alloc([8, 1024], F32)
        Rx1 = [S.R("x1_%d" % i) for i in range(2)]
        h2T = A.alloc([8, 1024], BF16)
        Rh2 = [S.R("h2_%d" % i) for i in range(2)]
        hid = A.alloc([22, 1024], BF16)
        Rhid = [S.R("hid%d" % i) for i in range(2)]
        NXR = 4
        xres = [A.alloc([512], F32) for _ in range(NXR)]
        Rxres = [S.R("xres%d" % i) for i in range(NXR)]
        sq = A.alloc([8, 512], BF16)
        Rsq = S.R("sq4")
        tmp = [A.alloc([512], F32) for _ in range(2)]
        Rtmp = [S.R("tmp0b"), S.R("tmp1b")]
        rl = A.alloc([512], F32)
        rstd = A.alloc([512], F32)
        Rrl, Rrstd = S.R("rl4"), S.R("rstd4")
        sg = [A.alloc([512], F32) for _ in range(2)]
        Rsg = [S.R("sg0"), S.R("sg1")]
        NSTG = 4
        stage = [A.alloc([512], F32) for _ in range(NSTG)]
        Rstage = [S.R("stage%d" % i) for i in range(NSTG)]
        NWFI, NWFO = 3, 2
        wfi = [A.alloc([8, 256], BF16) for _ in range(NWFI)]
        Rwfi = [S.R("wfi%d" % i) for i in range(NWFI)]
        wfo = [A.alloc([22, 128], BF16) for _ in range(NWFO)]
        Rwfo = [S.R("wfo%d" % i) for i in range(NWFO)]
        xr_i = 0
        TP4 = (sq, Rsq, rl, Rrl, rstd, Rrstd, tmp, Rtmp)

        def adv(g, n):
            if g is not None:
                for _ in range(n):
                    next(g, None)

        def fin_gen(sc, t):
            tok = slice(sc * 1024 + t * 512, sc * 1024 + (t + 1) * 512)

            def fin_out(kc, tok=tok):
                st_, Rst_ = stage[kc % NSTG], Rstage[kc % NSTG]
                return st_, Rst_, (lambda: stores.append(DMA("sp", d_out[b, kc, :, tok], st_, [Rst_], [])))
            return norm_gen(x1T[:, :, t * 512:(t + 1) * 512], Rx1[t], 2, fin_out, None, None, TP4)

        def n2_gen(t):
            return norm_gen(x1T[:, :, t * 512:(t + 1) * 512], Rx1[t], 1,
                            (lambda kc: h2T[:, kc, t * 512:(t + 1) * 512]), Rh2[t], 3, TP4)

        pend = None
        for sc in range(2):
            for t in range(2):
                side = pend if t == 0 else n2_gen(0)
                pend = None
                for dc in range(8):
                    tok = slice(sc * 1024 + t * 512, sc * 1024 + (t + 1) * 512)
                    xr, Rxr = xres[xr_i % NXR], Rxres[xr_i % NXR]
                    xr_i += 1
                    DMA("sp", xr, d_x[b, dc, :, tok], [], [Rxr])
                    bk, Rk = gen_bank()
                    for kc in range(8):
                        MM(bk[:, :], w_out[:, dc, kc, :], yT[:, kc, tok], kc == 0, kc == 7, [RwoP[dc], RyT], [Rk])
                    STT(x1T[:, dc, t * 512:(t + 1) * 512], bk[:, :], modT[:, 16 + dc, b:b + 1], xr, ALU.mult, ALU.add,
                        [Rk, Rmod, Rxr], [Rx1[t]])
                    adv(side, 2)
                if side is not None:
                    run_all(side)
            if debug and b == 0 and sc == 0:
                stores.append(DMA("sp", dbg["dbg_x1"], x1T, Rx1, []))
            side = n2_gen(1)
            order = [(s_, 0) for s_ in range(3)] + [(s_, 1) for s_ in range(3)] + \
                    [(s_, t_) for s_ in range(3, 22) for t_ in range(2)]
            loaded = set()
            for (s_, t) in order:
                wf, Rwf = wfi[s_ % NWFI], Rwfi[s_ % NWFI]
                if s_ not in loaded:
                    loaded.add(s_)
                    DMA("pool", wf, d_wffi[:, s_], [], [Rwf])
                if (s_, t) == (0, 1):
                    run_all(side)
                    side = None
                tl = slice(t * 512, (t + 1) * 512)
                bg, Rbg = gen_bank()
                for kc in range(8):
                    MM(bg[:, :], wf[:, kc, 0:128], h2T[:, kc, tl], kc == 0, kc == 7, [Rwf, Rh2[t]], [Rbg])
                bu, Rbu = gen_bank()
                for kc in range(8):
                    MM(bu[:, :], wf[:, kc, 128:256], h2T[:, kc, tl], kc == 0, kc == 7, [Rwf, Rh2[t]], [Rbu])
                g_, Rg_ = sg[(2 * s_ + t) % 2], Rsg[(2 * s_ + t) % 2]
                ACT(g_, bg[:, :], AF.Silu, [Rbg], [Rg_])
                TT("dve", hid[:, s_, tl], g_, bu[:, :], ALU.mult, [Rg_, Rbu], [Rhid[t]])
                adv(side, 4)
            for dc in range(8):
                wf, Rwf = wfo[dc % NWFO], Rwfo[dc % NWFO]
                DMA("pool", wf, d_wffo[:, dc], [], [Rwf])
                for t in range(2):
                    tl = slice(t * 512, (t + 1) * 512)
                    bk, Rk = gen_bank()
                    for s_ in range(22):
                        MM(bk[:, :], wf[:, s_, :], hid[:, s_, tl], s_ == 0, s_ == 21, [Rwf, Rhid[t]], [Rk])
                    STT(x1T[:, dc, tl], bk[:, :], modT[:, 40 + dc, b:b + 1], x1T[:, dc, tl], ALU.mult, ALU.add,
                        [Rk, Rmod, Rx1[t]], [Rx1[t]])
            run_all(fin_gen(sc, 0))
            pend = fin_gen(sc, 1)
        run_all(pend)
        S.barrier()
        A.top = M0

    print("arena peak", A.peak, "ops", {e: len(S.ops[e]) for e in ENGS})
    S.emit(final_waits=stores)
    return nc


def _t5_bucket(dist):
    n_buckets, max_distance = 32, 2048
    max_exact = n_buckets // 2
    d = np.maximum(dist, 1).astype(np.float64)
    large = max_exact + (np.log(d / max_exact) / np.log(max_distance / max_exact) * (n_buckets - max_exact)).astype(np.int64)
    large = np.minimum(large, n_buckets - 1)
    return np.where(dist < max_exact, dist, large).astype(np.int32)


def _host_consts():
    ident = np.eye(128, dtype=np.float32)
    ones = np.ones((128, 128), np.float32)
    tri = (np.arange(128)[None, :] >= np.arange(128)[:, None]).astype(np.float32)
    consts = np.concatenate([ident, ones, tri, ident[::-1]], axis=1)
    delta = np.arange(S_LEN)
    mult = (delta <= 128).astype(np.float32) + ((delta % 4 == 0) & (delta <= 512)) + ((delta % 16 == 0) & (delta <= 2048))
    onehot = np.zeros((32, S_LEN), np.float32)
    onehot[_t5_bucket(delta), delta] = mult
    half = 16
    inv = (np.float32(10000.0) ** (-np.arange(half, dtype=np.float32) / np.float32(half))).astype(np.float32)
    ang = (np.arange(S_LEN, dtype=np.float32)[:, None] * inv[None, :]).astype(np.float32)
    cos, sin = np.cos(ang).astype(np.float32).T, np.sin(ang).astype(np.float32).T
    rope = np.zeros((128, 2, S_LEN), np.float32)
    for g4 in range(4):
        rope[32 * g4:32 * g4 + 16, 0], rope[32 * g4 + 16:32 * g4 + 32, 0] = cos, cos
        rope[32 * g4:32 * g4 + 16, 1], rope[32 * g4 + 16:32 * g4 + 32, 1] = -sin, sin
    return consts, onehot, rope


def _fm(v, n):
    return np.ascontiguousarray(np.asarray(v, np.float32).reshape(n, 128).T)


def _prep_inputs(x, c, w_ada, b_ada, g_norm1, w_in, g_cq, w_uq, g_ckv, w_ukv, rel_bias,
                 g_out_a, g_out_b, w_out, g_norm2, w_ffn_in, w_ffn_out, g_final):
    f = lambda a: np.asarray(a, np.float32)
    x, c = f(x), f(c)
    consts, onehot, rope = _host_consts()
    shared = {}
    shared["w_ada"] = np.ascontiguousarray(f(w_ada)[0].reshape(8, 128, 6, 1024).transpose(1, 2, 0, 3))
    shared["b_adaT"] = _fm(f(b_ada)[0], 48)
    gv = np.zeros((128, 40), np.float32)
    gv[:, 0:8] = _fm(f(g_norm1)[0], 8)
    gv[:, 8:16] = _fm(f(g_norm2)[0], 8)
    gv[:, 16:24] = _fm(f(g_final), 8)
    gv[:, 24:28] = _fm(f(g_out_a)[0], 4)
    gv[:, 28:32] = _fm(f(g_out_b)[0], 4)
    gv[:, 32:35] = _fm(f(g_cq)[0], 3)
    gv[:, 35:37] = _fm(f(g_ckv)[0], 2)
    shared["gvec"] = gv
    wi = f(w_in)[0]
    kr = wi[:, 2176:2208]
    wi2 = np.concatenate([wi, kr[:, 16:32], kr[:, 0:16]], axis=1)
    shared["w_in"] = np.ascontiguousarray(wi2.reshape(8, 128, 2240).transpose(1, 0, 2))
    wq = f(w_uq)[0].reshape(384, 8, 96)
    nope = wq[:, :, :64].reshape(384, 512)
    ropec = wq[:, :, 64:].reshape(384, 256)
    ropesw = np.concatenate([wq[:, :, 80:96], wq[:, :, 64:80]], axis=2).reshape(384, 256)
    wq2 = np.concatenate([nope, ropec, ropesw], axis=1)
    shared["w_uq"] = np.ascontiguousarray(wq2.reshape(3, 128, 1024).transpose(1, 0, 2))
    wk = f(w_ukv)[0].reshape(256, 8, 2, 64)
    wk2 = np.concatenate([wk[:, :, 0].reshape(256, 512), wk[:, :, 1].reshape(256, 512)], axis=1)
    shared["w_ukv"] = np.ascontiguousarray(wk2.reshape(2, 128, 1024).transpose(1, 0, 2))
    shared["rel_bias"] = f(rel_bias)
    shared["w_out"] = np.ascontiguousarray(f(w_out)[0].reshape(8, 128, 8, 128).transpose(1, 2, 0, 3))
    wfi = f(w_ffn_in)[0]
    g_, u_ = wfi[:, :DFF].reshape(8, 128, 22, 128), wfi[:, DFF:].reshape(8, 128, 22, 128)
    gu = np.concatenate([g_, u_], axis=3)
    shared["w_ffn_in"] = np.ascontiguousarray(gu.transpose(1, 2, 0, 3))
    wfo = f(w_ffn_out)[0].reshape(22, 128, 8, 128)
    shared["w_ffn_out"] = np.ascontiguousarray(wfo.transpose(1, 2, 0, 3))
    shared["consts"] = consts
    shared["onehot"] = onehot
    shared["rope"] = rope
    in_maps = []
    for core in range(NCORES):
        xb = x[2 * core:2 * core + 2]
        m = dict(shared)
        m["xT"] = np.ascontiguousarray(xb.transpose(0, 2, 1).reshape(2, 8, 128, S_LEN))
        cb = c[2 * core:2 * core + 2]
        m["cT"] = np.ascontiguousarray(cb.reshape(2, 8, 128).transpose(2, 1, 0))
        in_maps.append(m)
    return in_maps


_NC_CACHE = {}


def kernel(**inputs):
    in_maps = _prep_inputs(**inputs)
    if "nc" not in _NC_CACHE:
        _NC_CACHE["nc"] = build(debug=False)
    res = run_bass_kernel_spmd(_NC_CACHE["nc"], in_maps, core_ids=list(range(NCORES)))
    out = np.empty((16, S_LEN, D), np.float32)
    for core in range(NCORES):
        o = np.asarray(res.results[core]["outT"], np.float32).reshape(2, D, S_LEN)
        out[2 * core:2 * core + 2] = o.transpose(0, 2, 1)
    return out
```
